# Optimizing a Trainium2 kernel written in Bass

```python
import jax, jax.numpy as jnp
from jax import lax
import numpy as np

D_MODEL = 2048
BATCH = 8
SEQ = 2048
DEPTH = 4

N_META = 16
N_CONV_LAYERS = max(DEPTH // 2, 1)
N_ATTN_LAYERS = max(DEPTH - N_CONV_LAYERS, 1)
CONV_WIDTH = 3
N_HEADS = 16
HEAD_DIM = D_MODEL // N_HEADS
D_FF = ((8 * D_MODEL // 3 + 255) // 256) * 256
Q_BLOCK = 128
DEEPNORM_ALPHA = (2 * DEPTH) ** 0.25
DEEPNORM_BETA = (8 * DEPTH) ** -0.25
LN_EPS = 1e-5

kernel_name = "yoco_shortconv_stickbreaking_macaron"


def layer_norm(x, g, b):
    xf = x.astype(jnp.float32)
    mu = jnp.mean(xf, axis=-1, keepdims=True)
    var = jnp.mean(jnp.square(xf - mu), axis=-1, keepdims=True)
    y = (xf - mu) * lax.rsqrt(var + LN_EPS)
    return (y * g.astype(jnp.float32) + b.astype(jnp.float32)).astype(x.dtype)


def swiglu(x, w_in, w_out):
    hid = x @ w_in
    gate = hid[..., :D_FF]
    up = hid[..., D_FF:]
    return (jax.nn.silu(gate) * up) @ w_out


def short_conv_mixer(x, w_in, conv_w, conv_b, w_out):
    T = x.shape[1]
    D = x.shape[2]
    proj = x @ w_in
    b_gate = proj[..., :D]
    c_gate = proj[..., D:2 * D]
    u = proj[..., 2 * D:]
    v = c_gate * u
    vp = jnp.pad(v, ((0, 0), (CONV_WIDTH - 1, 0), (0, 0)))
    conv = vp[:, 0:T] * conv_w[0]
    for tap in range(1, CONV_WIDTH):
        conv = conv + vp[:, tap:tap + T] * conv_w[tap]
    conv = conv + conv_b
    return (b_gate * conv) @ w_out


def split_heads(t):
    bsz, T = t.shape[0], t.shape[1]
    return t.reshape(bsz, T, N_HEADS, HEAD_DIM).transpose(0, 2, 1, 3)


def block_bounds(T):
    bounds = [0]
    if T > N_META:
        bounds.append(N_META)
        pos = N_META + Q_BLOCK
        while pos < T:
            bounds.append(pos)
            pos += Q_BLOCK
    bounds.append(T)
    out = []
    for s, e in zip(bounds[:-1], bounds[1:]):
        if e > s:
            out.append((int(s), int(e)))
    return out


def stick_breaking_attention(q, k, v):
    T = q.shape[2]
    scale = HEAD_DIM ** -0.5
    outs = []
    for s, e in block_bounds(T):
        qb = q[:, :, s:e]
        kb = k[:, :, :e]
        vb = v[:, :, :e]
        z = jnp.einsum('bhqd,bhkd->bhqk', qb, kb).astype(jnp.float32) * scale
        tq = jnp.arange(s, e, dtype=jnp.int32)[:, None]
        tk = jnp.arange(0, e, dtype=jnp.int32)[None, :]
        causal = tk < tq
        ls_neg = jax.nn.log_sigmoid(-z)
        log_keep = jnp.where(causal, ls_neg, 0.0)
        log_survive = lax.cumsum(log_keep, axis=3, reverse=True) - log_keep
        a = jnp.where(causal, jnp.exp(z + ls_neg + log_survive), 0.0)
        outs.append(jnp.einsum('bhqk,bhkd->bhqd', a.astype(vb.dtype), vb))
    return jnp.concatenate(outs, axis=2)


def setup_inputs(seed: int = 0) -> dict:
    key = jax.random.key(seed)
    ks = jax.random.split(key, 13)
    f32 = jnp.float32
    d, F = D_MODEL, D_FF
    return {
        "x": jax.random.normal(ks[0], (BATCH, SEQ, d), f32),
        "meta_tokens": jax.random.normal(ks[1], (N_META, d), f32),
        "ln_gain": 1.0 + 0.02 * jax.random.normal(ks[2], (DEPTH, 3, d), f32),
        "ln_bias": 0.02 * jax.random.normal(ks[3], (DEPTH, 3, d), f32),
        "ffn_w_in": jax.random.normal(ks[4], (DEPTH, 2, d, 2 * F), f32) * d ** -0.5,
        "ffn_w_out": jax.random.normal(ks[5], (DEPTH, 2, F, d), f32) * (F ** -0.5 * DEEPNORM_BETA),
        "conv_w_in": jax.random.normal(ks[6], (N_CONV_LAYERS, d, 3 * d), f32) * d ** -0.5,
        "conv_w": jax.random.normal(ks[7], (N_CONV_LAYERS, CONV_WIDTH, d), f32) * CONV_WIDTH ** -0.5,
        "conv_b": 0.02 * jax.random.normal(ks[8], (N_CONV_LAYERS, d), f32),
        "conv_w_out": jax.random.normal(ks[9], (N_CONV_LAYERS, d, d), f32) * (d ** -0.5 * DEEPNORM_BETA),
        "sb_w_q": jax.random.normal(ks[10], (N_ATTN_LAYERS, d, d), f32) * d ** -0.5,
        "sb_w_kv": jax.random.normal(ks[11], (d, 2 * d), f32) * d ** -0.5,
        "sb_w_o": jax.random.normal(ks[12], (N_ATTN_LAYERS, d, d), f32) * (d ** -0.5 * DEEPNORM_BETA),
    }


def reference(x, meta_tokens, ln_gain, ln_bias, ffn_w_in, ffn_w_out, conv_w_in, conv_w,
              conv_b, conv_w_out, sb_w_q, sb_w_kv, sb_w_o):
    bsz = x.shape[0]
    meta = jnp.broadcast_to(meta_tokens[None].astype(x.dtype), (bsz, N_META, x.shape[2]))
    h = jnp.concatenate([meta, x], axis=1)
    k_shared = None
    v_shared = None
    for i in range(DEPTH):
        h = layer_norm(DEEPNORM_ALPHA * h + 0.5 * swiglu(h, ffn_w_in[i, 0], ffn_w_out[i, 0]),
                       ln_gain[i, 0], ln_bias[i, 0])
        if i < N_CONV_LAYERS:
            mix = short_conv_mixer(h, conv_w_in[i], conv_w[i], conv_b[i], conv_w_out[i])
        else:
            j = i - N_CONV_LAYERS
            q = split_heads(h @ sb_w_q[j])
            o = stick_breaking_attention(q, k_shared, v_shared)
            o = o.transpose(0, 2, 1, 3).reshape(h.shape)
            mix = o @ sb_w_o[j]
        h = layer_norm(DEEPNORM_ALPHA * h + mix, ln_gain[i, 1], ln_bias[i, 1])
        h = layer_norm(DEEPNORM_ALPHA * h + 0.5 * swiglu(h, ffn_w_in[i, 1], ffn_w_out[i, 1]),
                       ln_gain[i, 2], ln_bias[i, 2])
        if i == N_CONV_LAYERS - 1:
            kv = h @ sb_w_kv
            k_shared = split_heads(kv[..., :D_MODEL])
            v_shared = split_heads(kv[..., D_MODEL:])
    return h[:, N_META:]
```

```python
import numpy as np
from contextlib import ExitStack
import concourse.bass as bass
import concourse.mybir as mybir
from concourse.bass_utils import run_bass_kernel_spmd

F32 = mybir.dt.float32
BF16 = mybir.dt.bfloat16
AF = mybir.ActivationFunctionType
ALU = mybir.AluOpType

D = 2048
NCH = 16
SEQ = 2048
NMETA = 16
T = SEQ + NMETA
DEPTH = 4
NCONV = 2
DFF = 5632
NFC = DFF // 128
NH = 16
HD = 128
ALPHA = (2 * DEPTH) ** 0.25
LN_EPS = 1e-5
SCALE = HD ** -0.5
SB = 1040
TILES = [(0, 16), (16, 528), (528, 1040)]
WSLOT = 8192
NWS = 3


class Sched:
    ENG = ("pe", "act", "dve", "pool", "sp")

    def __init__(self, nc, es):
        self.nc = nc
        self.es = es
        self.prog = {e: [] for e in self.ENG}
        self.esem = {e: es.enter_context(nc.semaphore("s_" + e)) for e in ("pe", "act", "dve", "pool")}
        self.ecnt = {e: 0 for e in self.esem}
        self.dsem = {}
        self.dcnt = {}
        self.known = {e: {} for e in self.ENG}
        self.cw = {}
        self.cr = {}
        self.nops = 0

    def _dma_sem(self, key):
        if key not in self.dsem:
            self.dsem[key] = self.es.enter_context(self.nc.semaphore("d_%d" % len(self.dsem)))
            self.dcnt[key] = 0
        return self.dsem[key]

    def op(self, eng, fn, reads=(), writes=(), dma=None, ndma=1):
        need = {}

        def add(tokd):
            for sid, (sem, val) in tokd.items():
                if need.get(sid, (None, 0))[1] < val:
                    need[sid] = (sem, val)

        for c in reads:
            add(self.cw.get(c, {}))
        for c in writes:
            add(self.cw.get(c, {}))
            add(self.cr.get(c, {}))
        waits = []
        for sid, (sem, val) in need.items():
            if eng == "pe" and dma is None and sem is self.esem["pe"]:
                continue
            if self.known[eng].get(sid, 0) < val:
                self.known[eng][sid] = val
                waits.append((sem, val))
        if dma is not None:
            sem = self._dma_sem(dma)
            self.dcnt[dma] += 16 * ndma
            tok = (sem, self.dcnt[dma])
            inc = 16
        else:
            sem = self.esem[eng]
            self.ecnt[eng] += 1
            tok = (sem, self.ecnt[eng])
            inc = 1
        sid = id(sem)
        self.prog[eng].append((waits, fn, sem, inc))
        for c in reads:
            d = self.cr.setdefault(c, {})
            d[sid] = tok
        for c in writes:
            self.cw[c] = {sid: tok}
            self.cr[c] = {}
        self.nops += 1

    def final_wait(self, eng, cells):
        need = {}
        for c in cells:
            for sid, (sem, val) in self.cw.get(c, {}).items():
                if need.get(sid, (None, 0))[1] < val:
                    need[sid] = (sem, val)
        self.prog[eng].append((list(need.values()), None, None, 0))

    def emit(self, block):
        nc = self.nc

        def run(e, items):
            for waits, fn, sem, inc in items:
                for wsem, val in waits:
                    e.wait_ge(wsem, val)
                if fn is None:
                    continue
                r = fn(e)
                if isinstance(r, (list, tuple)):
                    for ins in r:
                        ins.then_inc(sem, inc)
                else:
                    r.then_inc(sem, inc)

        @block.tensor
        def _(e):
            run(e, self.prog["pe"])

        @block.scalar
        def _(e):
            run(e, self.prog["act"])

        @block.vector
        def _(e):
            run(e, self.prog["dve"])

        @block.gpsimd
        def _(e):
            run(e, self.prog["pool"])

        @block.sync
        def _(e):
            run(e, self.prog["sp"])


class Rot:
    def __init__(self, items):
        self.items = list(items)
        self.i = 0

    def next(self):
        r = self.items[self.i % len(self.items)]
        self.i += 1
        return r


def build_program(nlayers=DEPTH, halves=(0, 1)):
    nc = bass.Bass("TRN2", target_bir_lowering=False)
    es = ExitStack()
    with es:
        dr = {}

        def din(name, shape):
            dr[name] = nc.dram_tensor(name, list(shape), F32, kind="ExternalInput").ap()
            return dr[name]

        x = din("x", [SEQ, D])
        meta = din("meta", [NMETA, D])
        prm_d = din("prm", [512, 128])
        cst_d = din("cst", [128, 5 * 128])
        w_ffn_in = din("ffn_w_in", [DEPTH * 2, D, 2 * DFF])
        w_ffn_out = din("ffn_w_out", [DEPTH * 2, DFF, D])
        w_cin = din("conv_w_in", [NCONV, D, 3 * D])
        w_cout = din("conv_w_out", [NCONV, D, D])
        w_q = din("sb_w_q", [2, D, D])
        w_kv = din("sb_w_kv", [D, 2 * D])
        w_o = din("sb_w_o", [2, D, D])
        y = nc.dram_tensor("y", [SEQ, D], F32, kind="ExternalOutput").ap()
        Kd = nc.dram_tensor("Kd", [NH, HD, T], BF16).ap()
        Vd = nc.dram_tensor("Vd", [NH, T, HD], BF16).ap()

        AW = 52200
        arena = es.enter_context(nc.sbuf_tensor("arena", [128, AW], F32))
        off = [0]

        def carve(nwords):
            a = off[0]
            off[0] += (nwords + 7) // 8 * 8
            assert off[0] <= AW, off[0]
            return arena[:, a:a + nwords]

        hf = carve(NCH * SB).rearrange("p (c t) -> p c t", c=NCH)
        hb = carve(NCH * SB // 2).bitcast(BF16).rearrange("p (c t) -> p c t", c=NCH)
        wsl = [carve(WSLOT // 2).bitcast(BF16) for _ in range(NWS)]
        gbuf = carve(8 * SB // 2).bitcast(BF16).rearrange("p (c t) -> p c t", c=8)
        NTMP = 8
        tmps = [carve(512) for _ in range(NTMP)]
        prm = carve(512)
        cstf = carve(2 * 128)
        cstb = carve(3 * 64).bitcast(BF16)
        zb = carve(64).bitcast(BF16)
        epsc = carve(8)
        eps_ap = epsc[:, 0:1]
        cstate = carve(NCONV * NCH * 2).rearrange("p (l c k) -> p l c k", l=NCONV, c=NCH)
        ovl0 = off[0]
        vext = [carve(SB + 2) for _ in range(2)]
        off[0] = ovl0
        qbuf = carve(4 * SB // 2).bitcast(BF16).rearrange("p (h t) -> p h t", h=4)
        kbuf = [carve(T // 2).bitcast(BF16) for _ in range(2)]
        vbuf = [carve(17 * 128 // 2).bitcast(BF16).rearrange("p (b d) -> p b d", b=17)]
        lsum = carve(256).bitcast(BF16)
        ident = cstf[:, 0:128]
        ones_f = cstf[:, 128:256]
        U_b = cstb[:, 0:128]
        tri_b = cstb[:, 128:256]
        ones_b = cstb[:, 256:384]

        banks = [es.enter_context(nc.psum_tensor("bank%d" % i, [128, 512], F32)) for i in range(8)]

        S = Sched(nc, es)
        brot = Rot(list(range(8)))
        trot = Rot(list(range(NTMP)))

        def PB(i):
            return ("bank", i)

        def TM(i):
            return ("tmp", i)

        units = []
        wstate = {"issued": 0}

        def colblock(wm, f0, w):
            return wm[:, f0:f0 + w].rearrange("(c p) f -> p c f", p=128)

        def rowblock(wm, r0, nr):
            return wm[r0 * 128:(r0 + nr) * 128, :].rearrange("(c p) f -> p c f", p=128)

        def issue_unit(i):
            parts = units[i]
            slot = i % NWS

            def fn(e, parts=parts, slot=slot):
                res = []
                for lo, c, w, src in parts:
                    dst = wsl[slot][:, lo:lo + c * w].rearrange("p (c f) -> p c f", c=c)
                    res.append(e.dma_start(out=dst, in_=src))
                return res

            S.op("pool", fn, reads=(), writes=[("ws", slot)], dma=("ws", slot), ndma=len(parts))

        def wget(i, ahead=NWS - 1):
            while wstate["issued"] < min(len(units), i + ahead + 1):
                issue_unit(wstate["issued"])
                wstate["issued"] += 1
            return i % NWS

        plan = {}

        def plan_units(key, lst):
            plan[key] = list(range(len(units), len(units) + len(lst)))
            units.extend(lst)

        def ffn_units(fi):
            lst = []
            wi, wo = w_ffn_in[fi], w_ffn_out[fi]
            win = lambda j: [(0, 16, 256, colblock(wi, j * 256, 256)),
                             (4096, 16, 256, colblock(wi, DFF + j * 256, 256))]
            wout = lambda b: [(0, 4, D, rowblock(wo, b * 4, 4))]
            order = []
            for blk in range(12):
                if blk < 11:
                    order.append(win(2 * blk))
                    order.append(win(2 * blk + 1))
                if blk >= 1:
                    order.append(wout(blk - 1))
            return order

        def conv_units(l):
            lst = []
            for half8 in range(2):
                for dc in range(half8 * 8, half8 * 8 + 8):
                    lst.append([(k * 2048, 16, 128, colblock(w_cin[l], k * D + dc * 128, 128)) for k in range(3)])
                for r in range(2):
                    lst.append([(0, 4, D, rowblock(w_cout[l], half8 * 8 + r * 4, 4))])
            return lst

        def attn_units(j):
            lst = []
            for half8 in range(2):
                for g in range(2):
                    lst.append([(0, 16, 512, colblock(w_q[j], (half8 * 2 + g) * 512, 512))])
                for r in range(2):
                    lst.append([(0, 4, D, rowblock(w_o[j], half8 * 8 + r * 4, 4))])
            return lst

        def kv_units():
            return [[(0, 16, 512, colblock(w_kv, g * 512, 512))] for g in range(8)]

        for hv in halves:
            for L in range(nlayers):
                plan_units((hv, L, "f0"), ffn_units(2 * L))
                if L < NCONV:
                    plan_units((hv, L, "mix"), conv_units(L))
                else:
                    plan_units((hv, L, "mix"), attn_units(L - NCONV))
                plan_units((hv, L, "f1"), ffn_units(2 * L + 1))
                if L == NCONV - 1:
                    plan_units((hv, L, "kv"), kv_units())

        def HF(c, ti):
            return ("hf", c, ti)

        def HB(c, ti):
            return ("hb", c, ti)

        def mm_group(bank, n, pairs, reads, M=128):
            def fn(e):
                last = None
                k = len(pairs)
                for i, (l, r) in enumerate(pairs):
                    last = e.matmul(banks[bank][0:M, 0:n], lhsT=l, rhs=r, start=(i == 0), stop=(i == k - 1))
                return last
            S.op("pe", fn, reads=reads, writes=[PB(bank)])


        def ACT(out, in_, func, reads, writes, **kw):
            S.op("act", lambda e: e.activation(out=out, in_=in_, func=func, **kw), reads=reads, writes=writes)

        def TT(eng, out, in0, in1, op, reads, writes):
            S.op(eng, lambda e: e.tensor_tensor(out=out, in0=in0, in1=in1, op=op), reads=reads, writes=writes)

        def TS(eng, out, in0, s1, s2, op0, op1, reads, writes):
            if op1 is None:
                S.op(eng, lambda e: e.tensor_scalar(out=out, in0=in0, scalar1=s1, scalar2=None, op0=op0), reads=reads, writes=writes)
            else:
                S.op(eng, lambda e: e.tensor_scalar(out=out, in0=in0, scalar1=s1, scalar2=s2, op0=op0, op1=op1), reads=reads, writes=writes)

        def STT(eng, out, in0, scalar, in1, op0, op1, reads, writes):
            S.op(eng, lambda e: e.scalar_tensor_tensor(out=out, in0=in0, scalar=scalar, in1=in1, op0=op0, op1=op1),
                 reads=reads, writes=writes)

        def COPY(eng, out, in_, reads, writes):
            S.op(eng, lambda e: e.tensor_copy(out=out, in_=in_), reads=reads, writes=writes)

        def MEMSET(eng, ap, val, writes):
            S.op(eng, lambda e: e.memset(ap, val), writes=writes)

        def DMA(eng, out, in_, reads, writes, key):
            S.op(eng, lambda e: e.dma_start(out=out, in_=in_), reads=reads, writes=writes, dma=key)

        def TRANSPOSE(out, in_, idn, reads, writes):
            S.op("pe", lambda e: e.transpose(out, in_, idn), reads=reads, writes=writes)

        def MM1(out, lhsT, rhs, start, stop, reads, writes):
            S.op("pe", lambda e: e.matmul(out, lhsT=lhsT, rhs=rhs, start=start, stop=stop, skip_group_check=True),
                 reads=reads, writes=writes)

        DMA("sp", cstf, cst_d[:, 0:256], [], ["cstf"], "cstf")
        DMA("pool", cstb, cst_d[:, 256:640], [], ["cstb"], "cstb")
        MEMSET("pool", zb, 0.0, ["zb"])
        MEMSET("pool", epsc, LN_EPS / (ALPHA * ALPHA), ["epsc"])
        for r in range(4):
            t = trot.next()
            DMA("sp", tmps[t][:, 0:128], prm_d[r * 128:(r + 1) * 128, :], [], [TM(t)], ("tmpld", t))
            b = brot.next()
            TRANSPOSE(banks[b][:, 0:128], tmps[t][:, 0:128], ident, [TM(t), "cstf"], [PB(b)])
            COPY("dve", prm[:, r * 128:(r + 1) * 128], banks[b][:, 0:128], [PB(b)], ["prm"])

        def P_LNG(L, s, c):
            i = (L * 3 + s) * 16 + c
            return prm[:, i:i + 1]

        def P_LNB(L, s, c):
            i = 192 + (L * 3 + s) * 16 + c
            return prm[:, i:i + 1]

        def P_CW(l, tap, c):
            i = 384 + (l * 3 + tap) * 16 + c
            return prm[:, i:i + 1]

        def P_CB(l, c):
            i = 480 + l * 16 + c
            return prm[:, i:i + 1]

        def load_state(hv):
            srcs = []
            if hv == 0:
                srcs.append((meta, 0, 16, 0, 0))
            for k in range(8):
                srcs.append((x, hv * 1024 + k * 128, 128, 16 + k * 128, 1 + k // 4))
            for (src, r0, nr, c0, ti) in srcs:
                for q in range(4):
                    t = trot.next()
                    DMA("sp", tmps[t][0:nr, :], src[r0:r0 + nr, q * 512:(q + 1) * 512], [], [TM(t)], ("tmpld", t))
                    for cc in range(4):
                        c = q * 4 + cc
                        b = brot.next()
                        TRANSPOSE(banks[b][:, 0:nr], tmps[t][0:nr, cc * 128:(cc + 1) * 128], ident[0:nr, 0:nr],
                                  [TM(t), "cstf"], [PB(b)])
                        ACT(hf[:, c, c0:c0 + nr], banks[b][:, 0:nr], AF.Copy, [PB(b)], [HF(c, ti)])
                        COPY("pool", hb[:, c, c0:c0 + nr], hf[:, c, c0:c0 + nr], [HF(c, ti)], [HB(c, ti)])

        outcells = []

        def store_out(hv):
            for k in range(8):
                ti = 1 + k // 4
                c0 = 16 + k * 128
                for q in range(4):
                    t = trot.next()
                    for cc in range(4):
                        c = q * 4 + cc
                        b = brot.next()
                        TRANSPOSE(banks[b][:, 0:128], hf[:, c, c0:c0 + 128], ident, [HF(c, ti), "cstf"], [PB(b)])
                        if cc % 2 == 0:
                            ACT(tmps[t][:, cc * 128:(cc + 1) * 128], banks[b][:, 0:128], AF.Copy, [PB(b)], [TM(t)])
                        else:
                            COPY("dve", tmps[t][:, cc * 128:(cc + 1) * 128], banks[b][:, 0:128], [PB(b)], [TM(t)])
                    r0 = hv * 1024 + k * 128
                    cell = ("y", hv, k, q)
                    DMA("sp", y[r0:r0 + 128, q * 512:(q + 1) * 512], tmps[t][:, :], [TM(t)], [cell], ("tmpst", t))
                    outcells.append(cell)

        def layer_norm(L, s, tis):
            eps = LN_EPS / (ALPHA * ALPHA)
            for ti in tis:
                a, bnd = TILES[ti]
                n = bnd - a
                b1 = brot.next()
                mm_group(b1, n, [(ones_f, hf[:, c, a:bnd]) for c in range(NCH)],
                         reads=[HF(c, ti) for c in range(NCH)] + ["cstf"])
                b2 = brot.next()
                while b2 == b1:
                    b2 = brot.next()
                for c in range(NCH):
                    t = trot.next()
                    ACT(tmps[t][:, 0:n], hf[:, c, a:bnd], AF.Square, [HF(c, ti)], [TM(t)])
                    MM1(banks[b2][:, 0:n], ones_f, tmps[t][:, 0:n], c == 0, c == NCH - 1, [TM(t), "cstf"], [PB(b2)])
                tm, tr, tq = trot.next(), trot.next(), trot.next()
                TS("dve", tmps[tm][:, 0:n], banks[b1][:, 0:n], 1.0 / D, None, ALU.mult, None, [PB(b1)], [TM(tm)])
                TT("dve", tmps[tq][:, 0:n], tmps[tm][:, 0:n], tmps[tm][:, 0:n], ALU.mult, [TM(tm)], [TM(tq)])
                STT("dve", tmps[tr][:, 0:n], banks[b2][:, 0:n], 1.0 / D, tmps[tq][:, 0:n], ALU.mult, ALU.subtract,
                    [PB(b2), TM(tq)], [TM(tr)])
                ACT(tmps[tr][:, 0:n], tmps[tr][:, 0:n], AF.Sqrt, [TM(tr), "epsc"], [TM(tr)], bias=eps_ap)
                S.op("dve", lambda e, o=tmps[tr][:, 0:n]: e.reciprocal(out=o, in_=o), reads=[TM(tr)], writes=[TM(tr)])
                for c in range(NCH):
                    t1 = trot.next()
                    while t1 in (tm, tr):
                        t1 = trot.next()
                    TT("dve", tmps[t1][:, 0:n], hf[:, c, a:bnd], tmps[tm][:, 0:n], ALU.subtract, [HF(c, ti), TM(tm)], [TM(t1)])
                    TT("pool", tmps[t1][:, 0:n], tmps[t1][:, 0:n], tmps[tr][:, 0:n], ALU.mult, [TM(t1), TM(tr)], [TM(t1)])
                    ACT(hf[:, c, a:bnd], tmps[t1][:, 0:n], AF.Identity, [TM(t1), "prm"], [HF(c, ti)],
                        bias=P_LNB(L, s, c), scale=P_LNG(L, s, c))
                    ACT(hb[:, c, a:bnd], tmps[t1][:, 0:n], AF.Identity, [TM(t1), "prm"], [HB(c, ti)],
                        bias=P_LNB(L, s, c), scale=P_LNG(L, s, c))

        def ffn(key, tis):
            uidx = list(plan[key])
            up_units = {}
            out_units = {}
            pos = 0
            for blk in range(12):
                if blk < 11:
                    up_units[2 * blk] = uidx[pos]; pos += 1
                    up_units[2 * blk + 1] = uidx[pos]; pos += 1
                if blk >= 1:
                    out_units[blk - 1] = uidx[pos]; pos += 1
            for blk in range(12):
                if blk < 11:
                    ms = blk % 2
                    for jj in range(2):
                        slot = wget(up_units[2 * blk + jj])
                        wv = wsl[slot].rearrange("p (g c f) -> p g c f", g=2, c=16)
                        for fcl in range(2):
                            for ti in tis:
                                a, bnd = TILES[ti]
                                n = bnd - a
                                bg = brot.next()
                                bu = brot.next()
                                rd = [HB(c, ti) for c in range(NCH)] + [("ws", slot)]
                                mm_group(bg, n, [(wv[:, 0, c, fcl * 128:(fcl + 1) * 128], hb[:, c, a:bnd]) for c in range(NCH)], rd)
                                mm_group(bu, n, [(wv[:, 1, c, fcl * 128:(fcl + 1) * 128], hb[:, c, a:bnd]) for c in range(NCH)], rd)
                                t = trot.next()
                                ACT(tmps[t][:, 0:n], banks[bg][:, 0:n], AF.Silu, [PB(bg)], [TM(t)])
                                mc = ms * 4 + jj * 2 + fcl
                                TT("dve", gbuf[:, mc, a:bnd], tmps[t][:, 0:n], banks[bu][:, 0:n], ALU.mult,
                                   [TM(t), PB(bu)], [("g", mc, ti)])
                if blk >= 1:
                    pb = blk - 1
                    ms = pb % 2
                    slot = wget(out_units[pb])
                    wv = wsl[slot].rearrange("p (k f) -> p k f", k=4)
                    for ti in tis:
                        a, bnd = TILES[ti]
                        n = bnd - a
                        for dc in range(NCH):
                            b = brot.next()
                            mm_group(b, n, [(wv[:, k, dc * 128:(dc + 1) * 128], gbuf[:, ms * 4 + k, a:bnd]) for k in range(4)],
                                     [("g", ms * 4 + k, ti) for k in range(4)] + [("ws", slot)])
                            STT("dve", hf[:, dc, a:bnd], banks[b][:, 0:n], 0.5 / ALPHA, hf[:, dc, a:bnd], ALU.mult, ALU.add,
                                [PB(b), HF(dc, ti)], [HF(dc, ti)])

        def out_proj(slots, tis):
            for ti in tis:
                a, bnd = TILES[ti]
                n = bnd - a
                for dco in range(NCH):
                    b = brot.next()
                    pairs = []
                    for k in range(8):
                        wv = wsl[slots[k // 4]].rearrange("p (k f) -> p k f", k=4)
                        pairs.append((wv[:, k % 4, dco * 128:(dco + 1) * 128], gbuf[:, k, a:bnd]))
                    mm_group(b, n, pairs, [("g", k, ti) for k in range(8)] + [("ws", s_) for s_ in slots])
                    STT("dve", hf[:, dco, a:bnd], banks[b][:, 0:n], 1.0 / ALPHA, hf[:, dco, a:bnd], ALU.mult, ALU.add,
                        [PB(b), HF(dco, ti)], [HF(dco, ti)])

        def conv_mixer(hv, l, key, tis):
            uidx = list(plan[key])
            pos = 0
            for half8 in range(2):
                for dcl in range(8):
                    dc = half8 * 8 + dcl
                    slot = wget(uidx[pos]); pos += 1
                    wv = wsl[slot][:, 0:6144].rearrange("p (g c f) -> p g c f", g=3, c=16)
                    vs = dc % 2
                    ve = vext[vs]
                    VC = ("vext", vs)
                    if hv == 0:
                        MEMSET("pool", ve[:, 0:2], 0.0, [VC])
                    else:
                        COPY("pool", ve[:, 16:18], cstate[:, l, dc, :], [("cstate", l, dc)], [VC])
                    for ti in tis:
                        a, bnd = TILES[ti]
                        n = bnd - a
                        rd = [HB(c, ti) for c in range(NCH)] + [("ws", slot)]
                        bc = brot.next()
                        bu = brot.next()
                        bb = brot.next()
                        for g, bk in ((1, bc), (2, bu), (0, bb)):
                            mm_group(bk, n, [(wv[:, g, c, :], hb[:, c, a:bnd]) for c in range(NCH)], rd)
                        t = trot.next()
                        ACT(tmps[t][:, 0:n], banks[bc][:, 0:n], AF.Copy, [PB(bc)], [TM(t)])
                        TT("dve", ve[:, 2 + a:2 + a + n], tmps[t][:, 0:n], banks[bu][:, 0:n], ALU.mult, [TM(t), PB(bu)], [VC])
                        t2 = trot.next()
                        TS("pool", tmps[t2][:, 0:n], ve[:, 2 + a:2 + a + n], P_CW(l, 2, dc), P_CB(l, dc), ALU.mult, ALU.add,
                           [VC, "prm"], [TM(t2)])
                        STT("dve", tmps[t2][:, 0:n], ve[:, 1 + a:1 + a + n], P_CW(l, 1, dc), tmps[t2][:, 0:n], ALU.mult, ALU.add,
                            [VC, "prm", TM(t2)], [TM(t2)])
                        STT("dve", tmps[t2][:, 0:n], ve[:, a:a + n], P_CW(l, 0, dc), tmps[t2][:, 0:n], ALU.mult, ALU.add,
                            [VC, "prm", TM(t2)], [TM(t2)])
                        TT("dve", gbuf[:, dcl, a:bnd], tmps[t2][:, 0:n], banks[bb][:, 0:n], ALU.mult, [TM(t2), PB(bb)], [("g", dcl, ti)])
                    if hv == 0:
                        COPY("pool", cstate[:, l, dc, :], ve[:, SB:SB + 2], [VC], [("cstate", l, dc)])
                s0 = wget(uidx[pos]); pos += 1
                s1 = wget(uidx[pos], ahead=1); pos += 1
                out_proj([s0, s1], tis)

        vd_cells = {}

        def kv_proj(hv, key, tis):
            uidx = list(plan[key])
            tok_base = 0 if hv == 0 else 1024
            for g in range(4):
                slot = wget(uidx[g])
                wv = wsl[slot].rearrange("p (c f) -> p c f", c=16)
                for hl in range(4):
                    h = 4 * g + hl
                    for ti in tis:
                        a, bnd = TILES[ti]
                        n = bnd - a
                        b = brot.next()
                        mm_group(b, n, [(wv[:, c, hl * 128:(hl + 1) * 128], hb[:, c, a:bnd]) for c in range(NCH)],
                                 [HB(c, ti) for c in range(NCH)] + [("ws", slot)])
                        t = trot.next()
                        tb = tmps[t].bitcast(BF16)
                        ACT(tb[:, 0:n], banks[b][:, 0:n], AF.Copy, [PB(b)], [TM(t)])
                        DMA("sp", Kd[h, :, tok_base + a:tok_base + a + n], tb[:, 0:n], [TM(t)], [("Kd", h, hv, ti)], ("tmpst", t))
            blocks = []
            for ti in tis:
                a, bnd = TILES[ti]
                for s0 in range(a, bnd, 128):
                    blocks.append((ti, s0, min(128, bnd - s0)))
            for g in range(4):
                slot = wget(uidx[4 + g])
                wv = wsl[slot].rearrange("p (c f) -> p c f", c=16)
                for (ti, s0, nt) in blocks:
                    b = brot.next()
                    mm_group(b, 512, [(hb[:, c, s0:s0 + nt], wv[:, c, :]) for c in range(NCH)],
                             [HB(c, ti) for c in range(NCH)] + [("ws", slot)], M=nt)
                    t = trot.next()
                    tb = tmps[t].bitcast(BF16)
                    COPY("dve", tb[0:nt, 0:512], banks[b][0:nt, 0:512], [PB(b)], [TM(t)])
                    gt = tok_base + s0
                    cell = ("Vd", g, hv, s0)
                    DMA("sp", Vd[4 * g:4 * g + 4, gt:gt + nt, :].rearrange("h t d -> t h d"),
                        tb[0:nt, 0:512].rearrange("t (h d) -> t h d", h=4), [TM(t)], [cell], ("tmpst", t))
                    vd_cells.setdefault(hv, []).append((g, cell))

        def attn_head_tile(hv, hl, ks, ti, hc):
            a, bnd = TILES[ti]
            qi = ti - 1 + 2 * hv
            ob = brot.next()
            MM1(banks[ob][:, 0:512], zb, hb[:, 0, 16:528], True, False, ["zb"], [PB(ob)])
            MEMSET("pool", lsum, 0.0, ["lsum"])
            kblocks = [("diag", 4 * qi + m, m) for m in (3, 2, 1, 0)]
            kblocks += [("full", kb, 0) for kb in range(4 * qi - 1, -1, -1)]
            kblocks.append(("meta", -1, 0))
            for (kind, kb, m) in kblocks:
                if kind == "meta":
                    k0, kp, vb = 0, 16, 0
                else:
                    k0, kp, vb = 16 + 128 * kb, 128, 1 + kb
                c0 = 128 * m if kind == "diag" else 0
                n = 512 - c0
                qsl = qbuf[:, hl, a + c0:bnd]
                ksl = kbuf[ks][:, k0:k0 + kp]
                zbk = brot.next()
                while zbk == ob:
                    zbk = brot.next()
                mm_group(zbk, n, [(ksl, qsl)], [("kbuf", ks), ("q", hl, ti)], M=kp)
                tE = trot.next()
                E = tmps[tE][0:kp, 0:n]
                ACT(E, banks[zbk][0:kp, 0:n], AF.Exp, [PB(zbk)], [TM(tE)])
                ACT(E, E, AF.Ln, [TM(tE)], [TM(tE)], bias=1.0)
                tL = trot.next()
                LA = tmps[tL].bitcast(BF16)
                Lb = LA[0:kp, 0:n]
                Ab = LA[0:kp, 512:512 + n]
                TS("pool", Lb, E, -1.0, None, ALU.mult, None, [TM(tE)], [TM(tL)])
                if kind == "diag":
                    TT("pool", LA[:, 0:128], LA[:, 0:128], tri_b, ALU.mult, [TM(tL), "cstb"], [TM(tL)])
                xb = brot.next()
                while xb in (ob, zbk):
                    xb = brot.next()
                pairs = [(ksl, qsl), (U_b[0:kp, 0:kp], Lb), (ones_b[:, 0:kp], lsum[:, c0:512])]
                mm_group(xb, n, pairs, [("kbuf", ks), ("q", hl, ti), TM(tL), "lsum", "cstb"], M=kp)
                if kind != "meta":
                    TT("pool", lsum[:, c0:512], lsum[:, c0:512], Lb, ALU.add, [TM(tL), "lsum"], ["lsum"])
                tX = trot.next()
                TT("dve", tmps[tX][0:kp, 0:n], banks[xb][0:kp, 0:n], E, ALU.subtract, [PB(xb), TM(tE)], [TM(tX)])
                ACT(Ab, tmps[tX][0:kp, 0:n], AF.Exp, [TM(tX)], [TM(tL)])
                if kind == "diag":
                    TT("pool", LA[:, 512:640], LA[:, 512:640], tri_b, ALU.mult, [TM(tL), "cstb"], [TM(tL)])
                MM1(banks[ob][:, c0:512], vbuf[0][0:kp, vb, :], Ab, False, kind == "meta", [TM(tL), ("vbuf", 0)], [PB(ob)])
            ACT(gbuf[:, hc, a:bnd], banks[ob][:, 0:512], AF.Copy, [PB(ob)], [("g", hc, ti)])

        def attn_mixer(hv, j, key):
            uidx = list(plan[key])
            pos = 0
            tis = [1, 2]
            tok_base = 0 if hv == 0 else 1024
            nkeys = tok_base + SB
            nblk = (nkeys - 16) // 128
            for half8 in range(2):
                for g2 in range(2):
                    g = half8 * 2 + g2
                    slot = wget(uidx[pos]); pos += 1
                    wv = wsl[slot].rearrange("p (c f) -> p c f", c=16)
                    for hl in range(4):
                        for ti in tis:
                            a, bnd = TILES[ti]
                            b = brot.next()
                            mm_group(b, 512, [(wv[:, c, hl * 128:(hl + 1) * 128], hb[:, c, a:bnd]) for c in range(NCH)],
                                     [HB(c, ti) for c in range(NCH)] + [("ws", slot)])
                            ACT(qbuf[:, hl, a:bnd], banks[b][:, 0:512], AF.Identity, [PB(b)], [("q", hl, ti)], scale=SCALE)
                    for hl in range(4):
                        h = 4 * g + hl
                        ks = h % 2
                        kcells = []
                        for hh in range(hv + 1):
                            for ti_ in ((0, 1, 2) if hh == 0 else (1, 2)):
                                kcells.append(("Kd", h, hh, ti_))
                        DMA("sp", kbuf[ks][:, 0:nkeys], Kd[h, :, 0:nkeys], kcells, [("kbuf", ks)], ("kbuf", ks))
                        vcells = [c_ for hh in range(hv + 1) for (gg, c_) in vd_cells[hh] if gg == h // 4]

                        def vfn(e, h=h):
                            return [e.dma_start(out=vbuf[0][0:16, 0, :], in_=Vd[h, 0:16, :]),
                                    e.dma_start(out=vbuf[0][:, 1:1 + nblk, :],
                                                in_=Vd[h, 16:nkeys, :].rearrange("(b p) d -> p b d", p=128))]
                        S.op("sp", vfn, reads=vcells, writes=[("vbuf", 0)], dma=("vbuf", 0), ndma=2)
                        for ti in tis:
                            attn_head_tile(hv, hl, ks, ti, g2 * 4 + hl)
                s0 = wget(uidx[pos]); pos += 1
                s1 = wget(uidx[pos], ahead=1); pos += 1
                out_proj([s0, s1], tis)

        for hv in halves:
            load_state(hv)
            for L in range(nlayers):
                tis = [0, 1, 2] if (hv == 0 and L < NCONV) else [1, 2]
                ffn((hv, L, "f0"), tis)
                layer_norm(L, 0, tis)
                if L < NCONV:
                    conv_mixer(hv, L, (hv, L, "mix"), tis)
                else:
                    attn_mixer(hv, L - NCONV, (hv, L, "mix"))
                layer_norm(L, 1, tis)
                ffn((hv, L, "f1"), tis)
                layer_norm(L, 2, tis)
                if L == NCONV - 1:
                    kv_proj(hv, (hv, L, "kv"), tis)
            store_out(hv)
        S.final_wait("sp", outcells)

        with nc.Block() as block:
            S.emit(block)
        stats = dict(nops=S.nops, ecnt=dict(S.ecnt), ndsem=len(S.dsem), words=off[0],
                     per_eng={e: len(v) for e, v in S.prog.items()})
    return nc, stats


def host_consts():
    ident = np.eye(128, dtype=np.float32)
    ones = np.ones((128, 128), np.float32)
    j = np.arange(128)[:, None]
    s = np.arange(128)[None, :]
    U = (j > s).astype(np.float32)
    tri = (s > j).astype(np.float32)
    return np.concatenate([ident, ones, U, tri, ones], axis=1)


def make_in_maps(x, meta_tokens, ln_gain, ln_bias, ffn_w_in, ffn_w_out, conv_w_in, conv_w, conv_b,
                 conv_w_out, sb_w_q, sb_w_kv, sb_w_o, ncores=8):
    f = lambda a: np.ascontiguousarray(np.asarray(a, dtype=np.float32))
    prm = np.concatenate([f(ln_gain).reshape(192, 128), f(ln_bias).reshape(192, 128),
                          f(conv_w).reshape(96, 128), f(conv_b).reshape(32, 128)], axis=0)
    shared = {
        "meta": f(meta_tokens), "prm": np.ascontiguousarray(prm), "cst": host_consts(),
        "ffn_w_in": f(ffn_w_in).reshape(DEPTH * 2, D, 2 * DFF), "ffn_w_out": f(ffn_w_out).reshape(DEPTH * 2, DFF, D),
        "conv_w_in": f(conv_w_in), "conv_w_out": f(conv_w_out), "sb_w_q": f(sb_w_q), "sb_w_kv": f(sb_w_kv),
        "sb_w_o": f(sb_w_o),
    }
    xs = f(x)
    return [dict(shared, x=xs[b]) for b in range(ncores)]


def kernel(x, meta_tokens, ln_gain, ln_bias, ffn_w_in, ffn_w_out, conv_w_in, conv_w, conv_b,
           conv_w_out, sb_w_q, sb_w_kv, sb_w_o):
    nc, _ = build_program()
    in_maps = make_in_maps(x, meta_tokens, ln_gain, ln_bias, ffn_w_in, ffn_w_out, conv_w_in, conv_w, conv_b,
                           conv_w_out, sb_w_q, sb_w_kv, sb_w_o)
    res = run_bass_kernel_spmd(nc, in_maps, core_ids=list(range(8)))
    return np.stack([r["y"] for r in res.results], axis=0).astype(np.float32)
```

```python
import numpy as np
from contextlib import ExitStack
import concourse.bass as bass
import concourse.mybir as mybir
from concourse.bass_utils import run_bass_kernel_spmd

F32 = mybir.dt.float32
BF16 = mybir.dt.bfloat16
AF = mybir.ActivationFunctionType
ALU = mybir.AluOpType

D = 2048
NCH = 16
SEQ = 2048
NMETA = 16
T = SEQ + NMETA
DEPTH = 4
NCONV = 2
DFF = 5632
NFC = DFF // 128
NH = 16
HD = 128
ALPHA = (2 * DEPTH) ** 0.25
LN_EPS = 1e-5
SCALE = HD ** -0.5
SB = 1040
TILES = [(0, 16), (16, 528), (528, 1040)]
WSLOT = 8192
NWS = 3


class Sched:
    ENG = ("pe", "act", "dve", "pool", "sp")

    def __init__(self, nc, es):
        self.nc = nc
        self.es = es
        self.prog = {e: [] for e in self.ENG}
        self.esem = {e: es.enter_context(nc.semaphore("s_" + e)) for e in ("pe", "act", "dve", "pool")}
        self.ecnt = {e: 0 for e in self.esem}
        self.dsem = {}
        self.dcnt = {}
        self.known = {e: {} for e in self.ENG}
        self.cw = {}
        self.cr = {}
        self.nops = 0

    def _dma_sem(self, key):
        if key not in self.dsem:
            self.dsem[key] = self.es.enter_context(self.nc.semaphore("d_%d" % len(self.dsem)))
            self.dcnt[key] = 0
        return self.dsem[key]

    def op(self, eng, fn, reads=(), writes=(), dma=None, ndma=1):
        need = {}

        def add(tokd):
            for sid, (sem, val) in tokd.items():
                if need.get(sid, (None, 0))[1] < val:
                    need[sid] = (sem, val)

        for c in reads:
            add(self.cw.get(c, {}))
        for c in writes:
            add(self.cw.get(c, {}))
            add(self.cr.get(c, {}))
        waits = []
        for sid, (sem, val) in need.items():
            if eng == "pe" and dma is None and sem is self.esem["pe"]:
                continue
            if self.known[eng].get(sid, 0) < val:
                self.known[eng][sid] = val
                waits.append((sem, val))
        if dma is not None:
            sem = self._dma_sem(dma)
            self.dcnt[dma] += 16 * ndma
            tok = (sem, self.dcnt[dma])
            inc = 16
        else:
            sem = self.esem[eng]
            self.ecnt[eng] += 1
            tok = (sem, self.ecnt[eng])
            inc = 1
        sid = id(sem)
        self.prog[eng].append((waits, fn, sem, inc))
        for c in reads:
            d = self.cr.setdefault(c, {})
            d[sid] = tok
        for c in writes:
            self.cw[c] = {sid: tok}
            self.cr[c] = {}
        self.nops += 1

    def final_wait(self, eng, cells):
        need = {}
        for c in cells:
            for sid, (sem, val) in self.cw.get(c, {}).items():
                if need.get(sid, (None, 0))[1] < val:
                    need[sid] = (sem, val)
        self.prog[eng].append((list(need.values()), None, None, 0))

    def emit(self, block):
        nc = self.nc

        def run(e, items):
            for waits, fn, sem, inc in items:
                for wsem, val in waits:
                    e.wait_ge(wsem, val)
                if fn is None:
                    continue
                r = fn(e)
                if isinstance(r, (list, tuple)):
                    for ins in r:
                        ins.then_inc(sem, inc)
                else:
                    r.then_inc(sem, inc)

        @block.tensor
        def _(e):
            run(e, self.prog["pe"])

        @block.scalar
        def _(e):
            run(e, self.prog["act"])

        @block.vector
        def _(e):
            run(e, self.prog["dve"])

        @block.gpsimd
        def _(e):
            run(e, self.prog["pool"])

        @block.sync
        def _(e):
            run(e, self.prog["sp"])


class Rot:
    def __init__(self, items):
        self.items = list(items)
        self.i = 0

    def next(self):
        r = self.items[self.i % len(self.items)]
        self.i += 1
        return r


def build_program(nlayers=DEPTH, halves=(0, 1)):
    nc = bass.Bass("TRN2", target_bir_lowering=False)
    es = ExitStack()
    with es:
        dr = {}

        def din(name, shape):
            dr[name] = nc.dram_tensor(name, list(shape), F32, kind="ExternalInput").ap()
            return dr[name]

        x = din("x", [SEQ, D])
        meta = din("meta", [NMETA, D])
        prm_d = din("prm", [512, 128])
        cst_d = din("cst", [128, 6 * 128])
        w_ffn_in = din("ffn_w_in", [DEPTH * 2, D, 2 * DFF])
        w_ffn_out = din("ffn_w_out", [DEPTH * 2, DFF, D])
        w_cin = din("conv_w_in", [NCONV, D, 3 * D])
        w_cout = din("conv_w_out", [NCONV, D, D])
        w_q = din("sb_w_q", [2, D, D])
        w_kv = din("sb_w_kv", [D, 2 * D])
        w_o = din("sb_w_o", [2, D, D])
        y = nc.dram_tensor("y", [SEQ, D], F32, kind="ExternalOutput").ap()
        Kd = nc.dram_tensor("Kd", [NH, HD, T], BF16).ap()
        Vd = nc.dram_tensor("Vd", [NH, T, HD], BF16).ap()

        AW = 52600
        arena = es.enter_context(nc.sbuf_tensor("arena", [128, AW], F32))
        off = [0]

        def carve(nwords):
            a = off[0]
            off[0] += (nwords + 7) // 8 * 8
            assert off[0] <= AW, off[0]
            return arena[:, a:a + nwords]

        hf = carve(NCH * SB).rearrange("p (c t) -> p c t", c=NCH)
        hb = carve(NCH * SB // 2).bitcast(BF16).rearrange("p (c t) -> p c t", c=NCH)
        wsl = [carve(WSLOT // 2).bitcast(BF16) for _ in range(NWS)]
        gbuf = carve(8 * SB // 2).bitcast(BF16).rearrange("p (c t) -> p c t", c=8)
        NTMP = 8
        tmps = [carve(512) for _ in range(NTMP)]
        prm = carve(512)
        cstf = carve(2 * 128)
        cstb = carve(4 * 64).bitcast(BF16)
        zb = carve(64).bitcast(BF16)
        epsc = carve(8)
        eps_ap = epsc[:, 0:1]
        cstate = carve(NCONV * NCH * 2).rearrange("p (l c k) -> p l c k", l=NCONV, c=NCH)
        ovl0 = off[0]
        vext = [carve(SB + 2) for _ in range(2)]
        off[0] = ovl0
        qbuf = carve(4 * SB // 2).bitcast(BF16).rearrange("p (h t) -> p h t", h=4)
        kbuf = [carve(T // 2).bitcast(BF16) for _ in range(2)]
        vbuf = [carve(17 * 128 // 2).bitcast(BF16).rearrange("p (b d) -> p b d", b=17)]
        lsums = [carve(256).bitcast(BF16) for _ in range(2)]
        ident = cstf[:, 0:128]
        ones_f = cstf[:, 128:256]
        U_b = cstb[:, 0:128]
        tri_b = cstb[:, 128:256]
        ones_b = cstb[:, 256:384]
        pones_b = cstb[:, 384:512]

        banks = [es.enter_context(nc.psum_tensor("bank%d" % i, [128, 512], F32)) for i in range(8)]

        S = Sched(nc, es)
        brot = Rot(list(range(8)))
        trot = Rot(list(range(NTMP)))
        lrot = Rot([0, 1])

        def PB(i):
            return ("bank", i)

        def TM(i):
            return ("tmp", i)

        units = []
        wstate = {"issued": 0}

        def colblock(wm, f0, w):
            return wm[:, f0:f0 + w].rearrange("(c p) f -> p c f", p=128)

        def rowblock(wm, r0, nr):
            return wm[r0 * 128:(r0 + nr) * 128, :].rearrange("(c p) f -> p c f", p=128)

        def issue_unit(i):
            parts = units[i]
            slot = i % NWS

            def fn(e, parts=parts, slot=slot):
                res = []
                for lo, c, w, src in parts:
                    dst = wsl[slot][:, lo:lo + c * w].rearrange("p (c f) -> p c f", c=c)
                    res.append(e.dma_start(out=dst, in_=src))
                return res

            S.op("pool", fn, reads=(), writes=[("ws", slot)], dma=("ws", slot), ndma=len(parts))

        def wget(i, ahead=NWS - 1):
            while wstate["issued"] < min(len(units), i + ahead + 1):
                issue_unit(wstate["issued"])
                wstate["issued"] += 1
            return i % NWS

        plan = {}

        def plan_units(key, lst):
            plan[key] = list(range(len(units), len(units) + len(lst)))
            units.extend(lst)

        def ffn_units(fi):
            lst = []
            wi, wo = w_ffn_in[fi], w_ffn_out[fi]
            win = lambda j: [(0, 16, 256, colblock(wi, j * 256, 256)),
                             (4096, 16, 256, colblock(wi, DFF + j * 256, 256))]
            wout = lambda b: [(0, 4, D, rowblock(wo, b * 4, 4))]
            order = []
            for blk in range(12):
                if blk < 11:
                    order.append(win(2 * blk))
                    order.append(win(2 * blk + 1))
                if blk >= 1:
                    order.append(wout(blk - 1))
            return order

        def conv_units(l):
            lst = []
            for half8 in range(2):
                for dc in range(half8 * 8, half8 * 8 + 8):
                    lst.append([(k * 2048, 16, 128, colblock(w_cin[l], k * D + dc * 128, 128)) for k in range(3)])
                for r in range(2):
                    lst.append([(0, 4, D, rowblock(w_cout[l], half8 * 8 + r * 4, 4))])
            return lst

        def attn_units(j):
            lst = []
            for half8 in range(2):
                for g in range(2):
                    lst.append([(0, 16, 512, colblock(w_q[j], (half8 * 2 + g) * 512, 512))])
                for r in range(2):
                    lst.append([(0, 4, D, rowblock(w_o[j], half8 * 8 + r * 4, 4))])
            return lst

        def kv_units():
            return [[(0, 16, 512, colblock(w_kv, g * 512, 512))] for g in range(8)]

        for hv in halves:
            for L in range(nlayers):
                plan_units((hv, L, "f0"), ffn_units(2 * L))
                if L < NCONV:
                    plan_units((hv, L, "mix"), conv_units(L))
                else:
                    plan_units((hv, L, "mix"), attn_units(L - NCONV))
                plan_units((hv, L, "f1"), ffn_units(2 * L + 1))
                if L == NCONV - 1:
                    plan_units((hv, L, "kv"), kv_units())

        def HF(c, ti):
            return ("hf", c, ti)

        def HB(c, ti):
            return ("hb", c, ti)

        def mm_group(bank, n, pairs, reads, M=128):
            def fn(e):
                last = None
                k = len(pairs)
                for i, (l, r) in enumerate(pairs):
                    last = e.matmul(banks[bank][0:M, 0:n], lhsT=l, rhs=r, start=(i == 0), stop=(i == k - 1))
                return last
            S.op("pe", fn, reads=reads, writes=[PB(bank)])


        def ACT(out, in_, func, reads, writes, **kw):
            S.op("act", lambda e: e.activation(out=out, in_=in_, func=func, **kw), reads=reads, writes=writes)

        def TT(eng, out, in0, in1, op, reads, writes):
            S.op(eng, lambda e: e.tensor_tensor(out=out, in0=in0, in1=in1, op=op), reads=reads, writes=writes)

        def TS(eng, out, in0, s1, s2, op0, op1, reads, writes):
            if op1 is None:
                S.op(eng, lambda e: e.tensor_scalar(out=out, in0=in0, scalar1=s1, scalar2=None, op0=op0), reads=reads, writes=writes)
            else:
                S.op(eng, lambda e: e.tensor_scalar(out=out, in0=in0, scalar1=s1, scalar2=s2, op0=op0, op1=op1), reads=reads, writes=writes)

        def STT(eng, out, in0, scalar, in1, op0, op1, reads, writes):
            S.op(eng, lambda e: e.scalar_tensor_tensor(out=out, in0=in0, scalar=scalar, in1=in1, op0=op0, op1=op1),
                 reads=reads, writes=writes)

        def COPY(eng, out, in_, reads, writes):
            S.op(eng, lambda e: e.tensor_copy(out=out, in_=in_), reads=reads, writes=writes)

        def MEMSET(eng, ap, val, writes):
            S.op(eng, lambda e: e.memset(ap, val), writes=writes)

        def DMA(eng, out, in_, reads, writes, key):
            S.op(eng, lambda e: e.dma_start(out=out, in_=in_), reads=reads, writes=writes, dma=key)

        def TRANSPOSE(out, in_, idn, reads, writes):
            S.op("pe", lambda e: e.transpose(out, in_, idn), reads=reads, writes=writes)

        def MM1(out, lhsT, rhs, start, stop, reads, writes):
            S.op("pe", lambda e: e.matmul(out, lhsT=lhsT, rhs=rhs, start=start, stop=stop, skip_group_check=True),
                 reads=reads, writes=writes)

        DMA("sp", cstf, cst_d[:, 0:256], [], ["cstf"], "cstf")
        DMA("pool", cstb, cst_d[:, 256:768], [], ["cstb"], "cstb")
        MEMSET("pool", zb, 0.0, ["zb"])
        MEMSET("pool", epsc, LN_EPS / (ALPHA * ALPHA), ["epsc"])
        for r in range(4):
            t = trot.next()
            DMA("sp", tmps[t][:, 0:128], prm_d[r * 128:(r + 1) * 128, :], [], [TM(t)], ("tmpld", t))
            b = brot.next()
            TRANSPOSE(banks[b][:, 0:128], tmps[t][:, 0:128], ident, [TM(t), "cstf"], [PB(b)])
            COPY("dve", prm[:, r * 128:(r + 1) * 128], banks[b][:, 0:128], [PB(b)], ["prm"])

        def P_LNG(L, s, c):
            i = (L * 3 + s) * 16 + c
            return prm[:, i:i + 1]

        def P_LNB(L, s, c):
            i = 192 + (L * 3 + s) * 16 + c
            return prm[:, i:i + 1]

        def P_CW(l, tap, c):
            i = 384 + (l * 3 + tap) * 16 + c
            return prm[:, i:i + 1]

        def P_CB(l, c):
            i = 480 + l * 16 + c
            return prm[:, i:i + 1]

        def load_state(hv):
            srcs = []
            if hv == 0:
                srcs.append((meta, 0, 16, 0, 0))
            for k in range(8):
                srcs.append((x, hv * 1024 + k * 128, 128, 16 + k * 128, 1 + k // 4))
            for (src, r0, nr, c0, ti) in srcs:
                for q in range(4):
                    t = trot.next()
                    DMA("sp", tmps[t][0:nr, :], src[r0:r0 + nr, q * 512:(q + 1) * 512], [], [TM(t)], ("tmpld", t))
                    for cc in range(4):
                        c = q * 4 + cc
                        b = brot.next()
                        TRANSPOSE(banks[b][:, 0:nr], tmps[t][0:nr, cc * 128:(cc + 1) * 128], ident[0:nr, 0:nr],
                                  [TM(t), "cstf"], [PB(b)])
                        ACT(hf[:, c, c0:c0 + nr], banks[b][:, 0:nr], AF.Copy, [PB(b)], [HF(c, ti)])
                        COPY("dve", hb[:, c, c0:c0 + nr], hf[:, c, c0:c0 + nr], [HF(c, ti)], [HB(c, ti)])

        outcells = []

        def store_out(hv):
            for k in range(8):
                ti = 1 + k // 4
                c0 = 16 + k * 128
                for q in range(4):
                    t = trot.next()
                    for cc in range(4):
                        c = q * 4 + cc
                        b = brot.next()
                        TRANSPOSE(banks[b][:, 0:128], hf[:, c, c0:c0 + 128], ident, [HF(c, ti), "cstf"], [PB(b)])
                        if cc % 2 == 0:
                            ACT(tmps[t][:, cc * 128:(cc + 1) * 128], banks[b][:, 0:128], AF.Copy, [PB(b)], [TM(t)])
                        else:
                            COPY("dve", tmps[t][:, cc * 128:(cc + 1) * 128], banks[b][:, 0:128], [PB(b)], [TM(t)])
                    r0 = hv * 1024 + k * 128
                    cell = ("y", hv, k, q)
                    DMA("sp", y[r0:r0 + 128, q * 512:(q + 1) * 512], tmps[t][:, :], [TM(t)], [cell], ("tmpst", t))
                    outcells.append(cell)

        def layer_norm(L, s, tis):
            eps = LN_EPS / (ALPHA * ALPHA)
            for ti in tis:
                a, bnd = TILES[ti]
                n = bnd - a
                b1 = brot.next()
                mm_group(b1, n, [(ones_f, hf[:, c, a:bnd]) for c in range(NCH)],
                         reads=[HF(c, ti) for c in range(NCH)] + ["cstf"])
                b2 = brot.next()
                while b2 == b1:
                    b2 = brot.next()
                for c in range(NCH):
                    t = trot.next()
                    sqb = tmps[t].bitcast(BF16)[:, 0:n]
                    ACT(sqb, hf[:, c, a:bnd], AF.Square, [HF(c, ti)], [TM(t)])
                    MM1(banks[b2][:, 0:n], pones_b, sqb, c == 0, c == NCH - 1, [TM(t), "cstb"], [PB(b2)])
                tm, tr, tq = trot.next(), trot.next(), trot.next()
                TS("dve", tmps[tm][:, 0:n], banks[b1][:, 0:n], 1.0 / D, None, ALU.mult, None, [PB(b1)], [TM(tm)])
                TT("dve", tmps[tq][:, 0:n], tmps[tm][:, 0:n], tmps[tm][:, 0:n], ALU.mult, [TM(tm)], [TM(tq)])
                STT("dve", tmps[tr][:, 0:n], banks[b2][:, 0:n], 1.0 / D, tmps[tq][:, 0:n], ALU.mult, ALU.subtract,
                    [PB(b2), TM(tq)], [TM(tr)])
                ACT(tmps[tr][:, 0:n], tmps[tr][:, 0:n], AF.Sqrt, [TM(tr), "epsc"], [TM(tr)], bias=eps_ap)
                S.op("dve", lambda e, o=tmps[tr][:, 0:n]: e.reciprocal(out=o, in_=o), reads=[TM(tr)], writes=[TM(tr)])
                for c in range(NCH):
                    t1 = trot.next()
                    while t1 in (tm, tr):
                        t1 = trot.next()
                    TT("dve", tmps[t1][:, 0:n], hf[:, c, a:bnd], tmps[tm][:, 0:n], ALU.subtract, [HF(c, ti), TM(tm)], [TM(t1)])
                    TT("dve", tmps[t1][:, 0:n], tmps[t1][:, 0:n], tmps[tr][:, 0:n], ALU.mult, [TM(t1), TM(tr)], [TM(t1)])
                    ACT(hf[:, c, a:bnd], tmps[t1][:, 0:n], AF.Identity, [TM(t1), "prm"], [HF(c, ti)],
                        bias=P_LNB(L, s, c), scale=P_LNG(L, s, c))
                    ACT(hb[:, c, a:bnd], tmps[t1][:, 0:n], AF.Identity, [TM(t1), "prm"], [HB(c, ti)],
                        bias=P_LNB(L, s, c), scale=P_LNG(L, s, c))

        def ffn(key, tis):
            uidx = list(plan[key])
            up_units = {}
            out_units = {}
            pos = 0
            for blk in range(12):
                if blk < 11:
                    up_units[2 * blk] = uidx[pos]; pos += 1
                    up_units[2 * blk + 1] = uidx[pos]; pos += 1
                if blk >= 1:
                    out_units[blk - 1] = uidx[pos]; pos += 1
            for blk in range(12):
                if blk < 11:
                    ms = blk % 2
                    for jj in range(2):
                        slot = wget(up_units[2 * blk + jj])
                        wv = wsl[slot].rearrange("p (g c f) -> p g c f", g=2, c=16)
                        for fcl in range(2):
                            for ti in tis:
                                a, bnd = TILES[ti]
                                n = bnd - a
                                bg = brot.next()
                                bu = brot.next()
                                rd = [HB(c, ti) for c in range(NCH)] + [("ws", slot)]
                                mm_group(bg, n, [(wv[:, 0, c, fcl * 128:(fcl + 1) * 128], hb[:, c, a:bnd]) for c in range(NCH)], rd)
                                mm_group(bu, n, [(wv[:, 1, c, fcl * 128:(fcl + 1) * 128], hb[:, c, a:bnd]) for c in range(NCH)], rd)
                                t = trot.next()
                                ACT(tmps[t][:, 0:n], banks[bg][:, 0:n], AF.Silu, [PB(bg)], [TM(t)])
                                mc = ms * 4 + jj * 2 + fcl
                                TT("dve", gbuf[:, mc, a:bnd], tmps[t][:, 0:n], banks[bu][:, 0:n], ALU.mult,
                                   [TM(t), PB(bu)], [("g", mc, ti)])
                if blk >= 1:
                    pb = blk - 1
                    ms = pb % 2
                    slot = wget(out_units[pb])
                    wv = wsl[slot].rearrange("p (k f) -> p k f", k=4)
                    for ti in tis:
                        a, bnd = TILES[ti]
                        n = bnd - a
                        for dc in range(NCH):
                            b = brot.next()
                            mm_group(b, n, [(wv[:, k, dc * 128:(dc + 1) * 128], gbuf[:, ms * 4 + k, a:bnd]) for k in range(4)],
                                     [("g", ms * 4 + k, ti) for k in range(4)] + [("ws", slot)])
                            STT("dve", hf[:, dc, a:bnd], banks[b][:, 0:n], 0.5 / ALPHA, hf[:, dc, a:bnd], ALU.mult, ALU.add,
                                [PB(b), HF(dc, ti)], [HF(dc, ti)])

        def out_proj(slots, tis):
            for ti in tis:
                a, bnd = TILES[ti]
                n = bnd - a
                for dco in range(NCH):
                    b = brot.next()
                    pairs = []
                    for k in range(8):
                        wv = wsl[slots[k // 4]].rearrange("p (k f) -> p k f", k=4)
                        pairs.append((wv[:, k % 4, dco * 128:(dco + 1) * 128], gbuf[:, k, a:bnd]))
                    mm_group(b, n, pairs, [("g", k, ti) for k in range(8)] + [("ws", s_) for s_ in slots])
                    STT("dve", hf[:, dco, a:bnd], banks[b][:, 0:n], 1.0 / ALPHA, hf[:, dco, a:bnd], ALU.mult, ALU.add,
                        [PB(b), HF(dco, ti)], [HF(dco, ti)])

        def conv_mixer(hv, l, key, tis):
            uidx = list(plan[key])
            pos = 0
            for half8 in range(2):
                for dcl in range(8):
                    dc = half8 * 8 + dcl
                    slot = wget(uidx[pos]); pos += 1
                    wv = wsl[slot][:, 0:6144].rearrange("p (g c f) -> p g c f", g=3, c=16)
                    vs = dc % 2
                    ve = vext[vs]
                    VC = ("vext", vs)
                    if hv == 0:
                        MEMSET("pool", ve[:, 0:2], 0.0, [VC])
                    else:
                        COPY("pool", ve[:, 16:18], cstate[:, l, dc, :], [("cstate", l, dc)], [VC])
                    for ti in tis:
                        a, bnd = TILES[ti]
                        n = bnd - a
                        rd = [HB(c, ti) for c in range(NCH)] + [("ws", slot)]
                        bc = brot.next()
                        bu = brot.next()
                        bb = brot.next()
                        for g, bk in ((1, bc), (2, bu), (0, bb)):
                            mm_group(bk, n, [(wv[:, g, c, :], hb[:, c, a:bnd]) for c in range(NCH)], rd)
                        t = trot.next()
                        ACT(tmps[t][:, 0:n], banks[bc][:, 0:n], AF.Copy, [PB(bc)], [TM(t)])
                        TT("dve", ve[:, 2 + a:2 + a + n], tmps[t][:, 0:n], banks[bu][:, 0:n], ALU.mult, [TM(t), PB(bu)], [VC])
                        t2 = trot.next()
                        TS("pool", tmps[t2][:, 0:n], ve[:, 2 + a:2 + a + n], P_CW(l, 2, dc), P_CB(l, dc), ALU.mult, ALU.add,
                           [VC, "prm"], [TM(t2)])
                        STT("dve", tmps[t2][:, 0:n], ve[:, 1 + a:1 + a + n], P_CW(l, 1, dc), tmps[t2][:, 0:n], ALU.mult, ALU.add,
                            [VC, "prm", TM(t2)], [TM(t2)])
                        STT("dve", tmps[t2][:, 0:n], ve[:, a:a + n], P_CW(l, 0, dc), tmps[t2][:, 0:n], ALU.mult, ALU.add,
                            [VC, "prm", TM(t2)], [TM(t2)])
                        TT("dve", gbuf[:, dcl, a:bnd], tmps[t2][:, 0:n], banks[bb][:, 0:n], ALU.mult, [TM(t2), PB(bb)], [("g", dcl, ti)])
                    if hv == 0:
                        COPY("pool", cstate[:, l, dc, :], ve[:, SB:SB + 2], [VC], [("cstate", l, dc)])
                s0 = wget(uidx[pos]); pos += 1
                s1 = wget(uidx[pos], ahead=1); pos += 1
                out_proj([s0, s1], tis)

        vd_cells = {}

        def kv_proj(hv, key, tis):
            uidx = list(plan[key])
            tok_base = 0 if hv == 0 else 1024
            for g in range(4):
                slot = wget(uidx[g])
                wv = wsl[slot].rearrange("p (c f) -> p c f", c=16)
                for hl in range(4):
                    h = 4 * g + hl
                    for ti in tis:
                        a, bnd = TILES[ti]
                        n = bnd - a
                        b = brot.next()
                        mm_group(b, n, [(wv[:, c, hl * 128:(hl + 1) * 128], hb[:, c, a:bnd]) for c in range(NCH)],
                                 [HB(c, ti) for c in range(NCH)] + [("ws", slot)])
                        t = trot.next()
                        tb = tmps[t].bitcast(BF16)
                        ACT(tb[:, 0:n], banks[b][:, 0:n], AF.Copy, [PB(b)], [TM(t)])
                        DMA("sp", Kd[h, :, tok_base + a:tok_base + a + n], tb[:, 0:n], [TM(t)], [("Kd", h, hv, ti)], ("tmpst", t))
            blocks = []
            for ti in tis:
                a, bnd = TILES[ti]
                for s0 in range(a, bnd, 128):
                    blocks.append((ti, s0, min(128, bnd - s0)))
            for g in range(4):
                slot = wget(uidx[4 + g])
                wv = wsl[slot].rearrange("p (c f) -> p c f", c=16)
                for (ti, s0, nt) in blocks:
                    b = brot.next()
                    mm_group(b, 512, [(hb[:, c, s0:s0 + nt], wv[:, c, :]) for c in range(NCH)],
                             [HB(c, ti) for c in range(NCH)] + [("ws", slot)], M=nt)
                    t = trot.next()
                    tb = tmps[t].bitcast(BF16)
                    COPY("dve", tb[0:nt, 0:512], banks[b][0:nt, 0:512], [PB(b)], [TM(t)])
                    gt = tok_base + s0
                    cell = ("Vd", g, hv, s0)
                    DMA("sp", Vd[4 * g:4 * g + 4, gt:gt + nt, :].rearrange("h t d -> t h d"),
                        tb[0:nt, 0:512].rearrange("t (h d) -> t h d", h=4), [TM(t)], [cell], ("tmpst", t))
                    vd_cells.setdefault(hv, []).append((g, cell))

        live_ob = set()

        def free_bank(excl=()):
            b = brot.next()
            while b in live_ob or b in excl:
                b = brot.next()
            return b

        def attn_jobs(hv, hl, ks, ti, hc, li):
            qi = ti - 1 + 2 * hv
            kblocks = [("diag", 4 * qi + m, m) for m in (3, 2, 1, 0)]
            kblocks += [("full", kb, 0) for kb in range(4 * qi - 1, -1, -1)]
            kblocks.append(("meta", -1, 0))
            jobs = []
            shared = {}
            for i, (kind, kb, m) in enumerate(kblocks):
                jobs.append(dict(kind=kind, kb=kb, m=m, hl=hl, ks=ks, ti=ti, hc=hc, li=li, first=(i == 0),
                                 last=(i == len(kblocks) - 1), sh=shared))
            return jobs

        def job_geom(J):
            a, bnd = TILES[J["ti"]]
            if J["kind"] == "meta":
                k0, kp, vb = 0, 16, 0
            else:
                k0, kp, vb = 16 + 128 * J["kb"], 128, 1 + J["kb"]
            c0 = 128 * J["m"] if J["kind"] == "diag" else 0
            n = 512 - c0
            qsl = qbuf[:, J["hl"], a + c0:bnd]
            ksl = kbuf[J["ks"]][:, k0:k0 + kp]
            return a, bnd, k0, kp, vb, c0, n, qsl, ksl

        def stage_A(J):
            a, bnd, k0, kp, vb, c0, n, qsl, ksl = job_geom(J)
            ls = lsums[J["li"]]
            if J["first"]:
                ob = free_bank()
                live_ob.add(ob)
                J["sh"]["ob"] = ob
                MM1(banks[ob][:, 0:512], zb, hb[:, 0, 16:528], True, False, ["zb"], [PB(ob)])
                MEMSET("dve", ls, 0.0, [("lsum", J["li"])])
            zbk = free_bank()
            mm_group(zbk, n, [(ksl, qsl)], [("kbuf", J["ks"]), ("q", J["hl"], J["ti"])], M=kp)
            tE = trot.next()
            tL = trot.next()
            J["tE"], J["tL"] = tE, tL
            E = tmps[tE][0:kp, 0:n]
            ACT(E, banks[zbk][0:kp, 0:n], AF.Exp, [PB(zbk)], [TM(tE)])
            ACT(E, E, AF.Ln, [TM(tE)], [TM(tE)], bias=1.0)
            LA = tmps[tL].bitcast(BF16)
            COPY("dve", LA[0:kp, 0:n], E, [TM(tE)], [TM(tL)])
            if J["kind"] == "diag":
                TT("dve", LA[:, 0:128], LA[:, 0:128], tri_b, ALU.mult, [TM(tL), "cstb"], [TM(tL)])

        def stage_B(J):
            a, bnd, k0, kp, vb, c0, n, qsl, ksl = job_geom(J)
            ls = lsums[J["li"]]
            LC = ("lsum", J["li"])
            tE, tL = J["tE"], J["tL"]
            E = tmps[tE][0:kp, 0:n]
            LA = tmps[tL].bitcast(BF16)
            SPb = LA[0:kp, 0:n]
            Ab = LA[0:kp, 512:512 + n]
            xb = free_bank()
            pairs = [(ksl, qsl), (U_b[0:kp, 0:kp], SPb), (ones_b[:, 0:kp], ls[:, c0:512])]
            mm_group(xb, n, pairs, [("kbuf", J["ks"]), ("q", J["hl"], J["ti"]), TM(tL), LC, "cstb"], M=kp)
            if J["kind"] != "meta":
                TT("dve", ls[:, c0:512], ls[:, c0:512], SPb, ALU.add, [TM(tL), LC], [LC])
            TT("dve", E, banks[xb][0:kp, 0:n], E, ALU.subtract, [PB(xb), TM(tE)], [TM(tE)])
            ACT(Ab, E, AF.Exp, [TM(tE)], [TM(tL)])
            if J["kind"] == "diag":
                TT("dve", LA[:, 512:640], LA[:, 512:640], tri_b, ALU.mult, [TM(tL), "cstb"], [TM(tL)])

        def stage_C(J):
            a, bnd, k0, kp, vb, c0, n, qsl, ksl = job_geom(J)
            ob = J["sh"]["ob"]
            LA = tmps[J["tL"]].bitcast(BF16)
            Ab = LA[0:kp, 512:512 + n]
            MM1(banks[ob][:, c0:512], vbuf[0][0:kp, vb, :], Ab, False, J["last"], [TM(J["tL"]), ("vbuf", 0)], [PB(ob)])
            if J["last"]:
                ACT(gbuf[:, J["hc"], a:bnd], banks[ob][:, 0:512], AF.Copy, [PB(ob)], [("g", J["hc"], J["ti"])])
                live_ob.discard(ob)

        def run_pipeline(jobs, preA, preC):
            nj = len(jobs)
            for i in range(nj + 2):
                if i < nj:
                    if i in preA:
                        preA[i]()
                    stage_A(jobs[i])
                if 1 <= i <= nj:
                    stage_B(jobs[i - 1])
                if 2 <= i:
                    if i - 2 in preC:
                        preC[i - 2]()
                    stage_C(jobs[i - 2])

        def attn_mixer(hv, j, key):
            uidx = list(plan[key])
            pos = 0
            tis = [1, 2]
            tok_base = 0 if hv == 0 else 1024
            nkeys = tok_base + SB
            nblk = (nkeys - 16) // 128
            for half8 in range(2):
                for g2 in range(2):
                    g = half8 * 2 + g2
                    slot = wget(uidx[pos]); pos += 1
                    wv = wsl[slot].rearrange("p (c f) -> p c f", c=16)
                    for hl in range(4):
                        for ti in tis:
                            a, bnd = TILES[ti]
                            b = brot.next()
                            mm_group(b, 512, [(wv[:, c, hl * 128:(hl + 1) * 128], hb[:, c, a:bnd]) for c in range(NCH)],
                                     [HB(c, ti) for c in range(NCH)] + [("ws", slot)])
                            ACT(qbuf[:, hl, a:bnd], banks[b][:, 0:512], AF.Identity, [PB(b)], [("q", hl, ti)], scale=SCALE)
                    jobs = []
                    preA = {}
                    preC = {}
                    for hl in range(4):
                        h = 4 * g + hl
                        ks = h % 2
                        kcells = []
                        for hh in range(hv + 1):
                            for ti_ in ((0, 1, 2) if hh == 0 else (1, 2)):
                                kcells.append(("Kd", h, hh, ti_))
                        vcells = [c_ for hh in range(hv + 1) for (gg, c_) in vd_cells[hh] if gg == h // 4]

                        def kload(h=h, ks=ks, kcells=kcells):
                            DMA("sp", kbuf[ks][:, 0:nkeys], Kd[h, :, 0:nkeys], kcells, [("kbuf", ks)], ("kbuf", ks))

                        def vload(h=h, vcells=vcells):
                            def vfn(e):
                                return [e.dma_start(out=vbuf[0][0:16, 0, :], in_=Vd[h, 0:16, :]),
                                        e.dma_start(out=vbuf[0][:, 1:1 + nblk, :],
                                                    in_=Vd[h, 16:nkeys, :].rearrange("(b p) d -> p b d", p=128))]
                            S.op("sp", vfn, reads=vcells, writes=[("vbuf", 0)], dma=("vbuf", 0), ndma=2)
                        preA[len(jobs)] = kload
                        preC[len(jobs)] = vload
                        for ti in tis:
                            jobs += attn_jobs(hv, hl, ks, ti, g2 * 4 + hl, lrot.next())
                    run_pipeline(jobs, preA, preC)
                s0 = wget(uidx[pos]); pos += 1
                s1 = wget(uidx[pos], ahead=1); pos += 1
                out_proj([s0, s1], tis)

        for hv in halves:
            load_state(hv)
            for L in range(nlayers):
                tis = [0, 1, 2] if (hv == 0 and L < NCONV) else [1, 2]
                ffn((hv, L, "f0"), tis)
                layer_norm(L, 0, tis)
                if L < NCONV:
                    conv_mixer(hv, L, (hv, L, "mix"), tis)
                else:
                    attn_mixer(hv, L - NCONV, (hv, L, "mix"))
                layer_norm(L, 1, tis)
                ffn((hv, L, "f1"), tis)
                layer_norm(L, 2, tis)
                if L == NCONV - 1:
                    kv_proj(hv, (hv, L, "kv"), tis)
            store_out(hv)
        S.final_wait("sp", outcells)

        with nc.Block() as block:
            S.emit(block)
        stats = dict(nops=S.nops, ecnt=dict(S.ecnt), ndsem=len(S.dsem), words=off[0],
                     per_eng={e: len(v) for e, v in S.prog.items()})
    return nc, stats


def host_consts():
    ident = np.eye(128, dtype=np.float32)
    ones = np.ones((128, 128), np.float32)
    j = np.arange(128)[:, None]
    s = np.arange(128)[None, :]
    U = (j > s).astype(np.float32)
    tri = (s > j).astype(np.float32)
    return np.concatenate([ident, ones, -U, tri, -ones, ones], axis=1)


def make_in_maps(x, meta_tokens, ln_gain, ln_bias, ffn_w_in, ffn_w_out, conv_w_in, conv_w, conv_b,
                 conv_w_out, sb_w_q, sb_w_kv, sb_w_o, ncores=8):
    f = lambda a: np.ascontiguousarray(np.asarray(a, dtype=np.float32))
    prm = np.concatenate([f(ln_gain).reshape(192, 128), f(ln_bias).reshape(192, 128),
                          f(conv_w).reshape(96, 128), f(conv_b).reshape(32, 128)], axis=0)
    shared = {
        "meta": f(meta_tokens), "prm": np.ascontiguousarray(prm), "cst": host_consts(),
        "ffn_w_in": f(ffn_w_in).reshape(DEPTH * 2, D, 2 * DFF), "ffn_w_out": f(ffn_w_out).reshape(DEPTH * 2, DFF, D),
        "conv_w_in": f(conv_w_in), "conv_w_out": f(conv_w_out), "sb_w_q": f(sb_w_q), "sb_w_kv": f(sb_w_kv),
        "sb_w_o": f(sb_w_o),
    }
    xs = f(x)
    return [dict(shared, x=xs[b]) for b in range(ncores)]


def kernel(x, meta_tokens, ln_gain, ln_bias, ffn_w_in, ffn_w_out, conv_w_in, conv_w, conv_b,
           conv_w_out, sb_w_q, sb_w_kv, sb_w_o):
    nc, _ = build_program()
    in_maps = make_in_maps(x, meta_tokens, ln_gain, ln_bias, ffn_w_in, ffn_w_out, conv_w_in, conv_w, conv_b,
                           conv_w_out, sb_w_q, sb_w_kv, sb_w_o)
    res = run_bass_kernel_spmd(nc, in_maps, core_ids=list(range(8)))
    return np.stack([r["y"] for r in res.results], axis=0).astype(np.float32)
```

```python
import numpy as np
from contextlib import ExitStack
import concourse.bass as bass
import concourse.mybir as mybir
from concourse.bass_utils import run_bass_kernel_spmd

F32 = mybir.dt.float32
BF16 = mybir.dt.bfloat16
AF = mybir.ActivationFunctionType
ALU = mybir.AluOpType

D = 2048
NCH = 16
SEQ = 2048
NMETA = 16
T = SEQ + NMETA
DEPTH = 4
NCONV = 2
DFF = 5632
NFC = DFF // 128
NH = 16
HD = 128
ALPHA = (2 * DEPTH) ** 0.25
LN_EPS = 1e-5
SCALE = HD ** -0.5
SB = 1040
TILES = [(0, 16), (16, 528), (528, 1040)]
WSLOT = 8192
NWS = 3


class Sched:
    ENG = ("pe", "act", "dve", "pool", "sp")

    def __init__(self, nc, es):
        self.nc = nc
        self.es = es
        self.prog = {e: [] for e in self.ENG}
        self.esem = {e: es.enter_context(nc.semaphore("s_" + e)) for e in ("pe", "act", "dve", "pool")}
        self.ecnt = {e: 0 for e in self.esem}
        self.dsem = {}
        self.dcnt = {}
        self.known = {e: {} for e in self.ENG}
        self.cw = {}
        self.cr = {}
        self.nops = 0

    def _dma_sem(self, key):
        if key not in self.dsem:
            self.dsem[key] = self.es.enter_context(self.nc.semaphore("d_%d" % len(self.dsem)))
            self.dcnt[key] = 0
        return self.dsem[key]

    def op(self, eng, fn, reads=(), writes=(), dma=None, ndma=1):
        need = {}

        def add(tokd):
            for sid, (sem, val) in tokd.items():
                if need.get(sid, (None, 0))[1] < val:
                    need[sid] = (sem, val)

        for c in reads:
            add(self.cw.get(c, {}))
        for c in writes:
            add(self.cw.get(c, {}))
            add(self.cr.get(c, {}))
        waits = []
        for sid, (sem, val) in need.items():
            if eng == "pe" and dma is None and sem is self.esem["pe"]:
                continue
            if self.known[eng].get(sid, 0) < val:
                self.known[eng][sid] = val
                waits.append((sem, val))
        if dma is not None:
            sem = self._dma_sem(dma)
            self.dcnt[dma] += 16 * ndma
            tok = (sem, self.dcnt[dma])
            inc = 16
        else:
            sem = self.esem[eng]
            self.ecnt[eng] += 1
            tok = (sem, self.ecnt[eng])
            inc = 1
        sid = id(sem)
        self.prog[eng].append((waits, fn, sem, inc))
        for c in reads:
            d = self.cr.setdefault(c, {})
            d[sid] = tok
        for c in writes:
            self.cw[c] = {sid: tok}
            self.cr[c] = {}
        self.nops += 1

    def final_wait(self, eng, cells):
        need = {}
        for c in cells:
            for sid, (sem, val) in self.cw.get(c, {}).items():
                if need.get(sid, (None, 0))[1] < val:
                    need[sid] = (sem, val)
        self.prog[eng].append((list(need.values()), None, None, 0))

    def emit(self, block):
        nc = self.nc

        def run(e, items):
            for waits, fn, sem, inc in items:
                for wsem, val in waits:
                    e.wait_ge(wsem, val)
                if fn is None:
                    continue
                r = fn(e)
                if isinstance(r, (list, tuple)):
                    for ins in r:
                        ins.then_inc(sem, inc)
                else:
                    r.then_inc(sem, inc)

        @block.tensor
        def _(e):
            run(e, self.prog["pe"])

        @block.scalar
        def _(e):
            run(e, self.prog["act"])

        @block.vector
        def _(e):
            run(e, self.prog["dve"])

        @block.gpsimd
        def _(e):
            run(e, self.prog["pool"])

        @block.sync
        def _(e):
            run(e, self.prog["sp"])


class Rot:
    def __init__(self, items):
        self.items = list(items)
        self.i = 0

    def next(self):
        r = self.items[self.i % len(self.items)]
        self.i += 1
        return r


def build_program(nlayers=DEPTH, halves=(0, 1)):
    nc = bass.Bass("TRN2", target_bir_lowering=False)
    es = ExitStack()
    with es:
        dr = {}

        def din(name, shape):
            dr[name] = nc.dram_tensor(name, list(shape), F32, kind="ExternalInput").ap()
            return dr[name]

        x = din("x", [SEQ, D])
        meta = din("meta", [NMETA, D])
        prm_d = din("prm", [512, 128])
        cst_d = din("cst", [128, 6 * 128])
        w_ffn_in = din("ffn_w_in", [DEPTH * 2, D, 2 * DFF])
        w_ffn_out = din("ffn_w_out", [DEPTH * 2, DFF, D])
        w_cin = din("conv_w_in", [NCONV, D, 3 * D])
        w_cout = din("conv_w_out", [NCONV, D, D])
        w_q = din("sb_w_q", [2, D, D])
        w_kv = din("sb_w_kv", [D, 2 * D])
        w_o = din("sb_w_o", [2, D, D])
        y = nc.dram_tensor("y", [SEQ, D], F32, kind="ExternalOutput").ap()
        Kd = nc.dram_tensor("Kd", [NH, HD, T], BF16).ap()
        Vd = nc.dram_tensor("Vd", [NH, T, HD], BF16).ap()

        AW = 52600
        arena = es.enter_context(nc.sbuf_tensor("arena", [128, AW], F32))
        off = [0]

        def carve(nwords):
            a = off[0]
            off[0] += (nwords + 7) // 8 * 8
            assert off[0] <= AW, off[0]
            return arena[:, a:a + nwords]

        hf = carve(NCH * SB).rearrange("p (c t) -> p c t", c=NCH)
        hb = carve(NCH * SB // 2).bitcast(BF16).rearrange("p (c t) -> p c t", c=NCH)
        wsl = [carve(WSLOT // 2).bitcast(BF16) for _ in range(NWS)]
        gbuf = carve(8 * SB // 2).bitcast(BF16).rearrange("p (c t) -> p c t", c=8)
        NTMP = 8
        tmps = [carve(512) for _ in range(NTMP)]
        prm = carve(512)
        cstf = carve(2 * 128)
        cstb = carve(4 * 64).bitcast(BF16)
        zb = carve(64).bitcast(BF16)
        epsc = carve(8)
        eps_ap = epsc[:, 0:1]
        cstate = carve(NCONV * NCH * 2).rearrange("p (l c k) -> p l c k", l=NCONV, c=NCH)
        ovl0 = off[0]
        vext = [carve(SB + 2) for _ in range(2)]
        off[0] = ovl0
        qbuf = carve(4 * SB // 2).bitcast(BF16).rearrange("p (h t) -> p h t", h=4)
        kbuf = [carve(T // 2).bitcast(BF16) for _ in range(2)]
        vbuf = [carve(17 * 128 // 2).bitcast(BF16).rearrange("p (b d) -> p b d", b=17)]
        lsums = [carve(256).bitcast(BF16) for _ in range(2)]
        ident = cstf[:, 0:128]
        ones_f = cstf[:, 128:256]
        U_b = cstb[:, 0:128]
        tri_b = cstb[:, 128:256]
        ones_b = cstb[:, 256:384]
        pones_b = cstb[:, 384:512]

        banks = [es.enter_context(nc.psum_tensor("bank%d" % i, [128, 512], F32)) for i in range(8)]

        S = Sched(nc, es)
        brot = Rot(list(range(8)))
        trot = Rot(list(range(NTMP)))
        lrot = Rot([0, 1])

        def PB(i):
            return ("bank", i)

        def TM(i):
            return ("tmp", i)

        units = []
        wstate = {"issued": 0}

        def colblock(wm, f0, w):
            return wm[:, f0:f0 + w].rearrange("(c p) f -> p c f", p=128)

        def rowblock(wm, r0, nr):
            return wm[r0 * 128:(r0 + nr) * 128, :].rearrange("(c p) f -> p c f", p=128)

        def issue_unit(i):
            parts = units[i]
            slot = i % NWS

            def fn(e, parts=parts, slot=slot):
                res = []
                for lo, c, w, src in parts:
                    dst = wsl[slot][:, lo:lo + c * w].rearrange("p (c f) -> p c f", c=c)
                    res.append(e.dma_start(out=dst, in_=src))
                return res

            S.op("pool", fn, reads=(), writes=[("ws", slot)], dma=("ws", slot), ndma=len(parts))

        def wget(i, ahead=NWS - 1):
            while wstate["issued"] < min(len(units), i + ahead + 1):
                issue_unit(wstate["issued"])
                wstate["issued"] += 1
            return i % NWS

        plan = {}

        def plan_units(key, lst):
            plan[key] = list(range(len(units), len(units) + len(lst)))
            units.extend(lst)

        def ffn_units(fi):
            lst = []
            wi, wo = w_ffn_in[fi], w_ffn_out[fi]
            win = lambda j: [(0, 16, 256, colblock(wi, j * 256, 256)),
                             (4096, 16, 256, colblock(wi, DFF + j * 256, 256))]
            wout = lambda b: [(0, 4, D, rowblock(wo, b * 4, 4))]
            order = []
            for blk in range(12):
                if blk < 11:
                    order.append(win(2 * blk))
                    order.append(win(2 * blk + 1))
                if blk >= 1:
                    order.append(wout(blk - 1))
            return order

        def conv_units(l):
            lst = []
            for half8 in range(2):
                for dc in range(half8 * 8, half8 * 8 + 8):
                    lst.append([(k * 2048, 16, 128, colblock(w_cin[l], k * D + dc * 128, 128)) for k in range(3)])
                for r in range(2):
                    lst.append([(0, 4, D, rowblock(w_cout[l], half8 * 8 + r * 4, 4))])
            return lst

        def attn_units(j):
            lst = []
            for half8 in range(2):
                for g in range(2):
                    lst.append([(0, 16, 512, colblock(w_q[j], (half8 * 2 + g) * 512, 512))])
                for r in range(2):
                    lst.append([(0, 4, D, rowblock(w_o[j], half8 * 8 + r * 4, 4))])
            return lst

        def kv_units():
            return [[(0, 16, 512, colblock(w_kv, g * 512, 512))] for g in range(8)]

        for hv in halves:
            for L in range(nlayers):
                plan_units((hv, L, "f0"), ffn_units(2 * L))
                if L < NCONV:
                    plan_units((hv, L, "mix"), conv_units(L))
                else:
                    plan_units((hv, L, "mix"), attn_units(L - NCONV))
                plan_units((hv, L, "f1"), ffn_units(2 * L + 1))
                if L == NCONV - 1:
                    plan_units((hv, L, "kv"), kv_units())

        def HF(c, ti):
            return ("hf", c, ti)

        def HB(c, ti):
            return ("hb", c, ti)

        def mm_group(bank, n, pairs, reads, M=128):
            def fn(e):
                last = None
                k = len(pairs)
                for i, (l, r) in enumerate(pairs):
                    last = e.matmul(banks[bank][0:M, 0:n], lhsT=l, rhs=r, start=(i == 0), stop=(i == k - 1))
                return last
            S.op("pe", fn, reads=reads, writes=[PB(bank)])


        def mm_seq(bank, n, pairs, start_first, stop_last, reads, M=128):
            def fn(e):
                last = None
                k = len(pairs)
                for i, (l, r) in enumerate(pairs):
                    last = e.matmul(banks[bank][0:M, 0:n], lhsT=l, rhs=r, start=(start_first and i == 0),
                                    stop=(stop_last and i == k - 1), skip_group_check=True)
                return last
            S.op("pe", fn, reads=reads, writes=[PB(bank)])

        def ACT(out, in_, func, reads, writes, **kw):
            S.op("act", lambda e: e.activation(out=out, in_=in_, func=func, **kw), reads=reads, writes=writes)

        def TT(eng, out, in0, in1, op, reads, writes):
            S.op(eng, lambda e: e.tensor_tensor(out=out, in0=in0, in1=in1, op=op), reads=reads, writes=writes)

        def TS(eng, out, in0, s1, s2, op0, op1, reads, writes):
            if op1 is None:
                S.op(eng, lambda e: e.tensor_scalar(out=out, in0=in0, scalar1=s1, scalar2=None, op0=op0), reads=reads, writes=writes)
            else:
                S.op(eng, lambda e: e.tensor_scalar(out=out, in0=in0, scalar1=s1, scalar2=s2, op0=op0, op1=op1), reads=reads, writes=writes)

        def STT(eng, out, in0, scalar, in1, op0, op1, reads, writes):
            S.op(eng, lambda e: e.scalar_tensor_tensor(out=out, in0=in0, scalar=scalar, in1=in1, op0=op0, op1=op1),
                 reads=reads, writes=writes)

        def COPY(eng, out, in_, reads, writes):
            S.op(eng, lambda e: e.tensor_copy(out=out, in_=in_), reads=reads, writes=writes)

        def MEMSET(eng, ap, val, writes):
            S.op(eng, lambda e: e.memset(ap, val), writes=writes)

        def DMA(eng, out, in_, reads, writes, key):
            S.op(eng, lambda e: e.dma_start(out=out, in_=in_), reads=reads, writes=writes, dma=key)

        def TRANSPOSE(out, in_, idn, reads, writes):
            S.op("pe", lambda e: e.transpose(out, in_, idn), reads=reads, writes=writes)

        def MM1(out, lhsT, rhs, start, stop, reads, writes):
            S.op("pe", lambda e: e.matmul(out, lhsT=lhsT, rhs=rhs, start=start, stop=stop, skip_group_check=True),
                 reads=reads, writes=writes)

        DMA("sp", cstf, cst_d[:, 0:256], [], ["cstf"], "cstf")
        DMA("pool", cstb, cst_d[:, 256:768], [], ["cstb"], "cstb")
        MEMSET("pool", zb, 0.0, ["zb"])
        MEMSET("pool", epsc, LN_EPS / (ALPHA * ALPHA), ["epsc"])
        for r in range(4):
            t = trot.next()
            DMA("sp", tmps[t][:, 0:128], prm_d[r * 128:(r + 1) * 128, :], [], [TM(t)], ("tmpld", t))
            b = brot.next()
            TRANSPOSE(banks[b][:, 0:128], tmps[t][:, 0:128], ident, [TM(t), "cstf"], [PB(b)])
            COPY("dve", prm[:, r * 128:(r + 1) * 128], banks[b][:, 0:128], [PB(b)], ["prm"])

        def P_LNG(L, s, c):
            i = (L * 3 + s) * 16 + c
            return prm[:, i:i + 1]

        def P_LNB(L, s, c):
            i = 192 + (L * 3 + s) * 16 + c
            return prm[:, i:i + 1]

        def P_CW(l, tap, c):
            i = 384 + (l * 3 + tap) * 16 + c
            return prm[:, i:i + 1]

        def P_CB(l, c):
            i = 480 + l * 16 + c
            return prm[:, i:i + 1]

        def load_state(hv):
            srcs = []
            if hv == 0:
                srcs.append((meta, 0, 16, 0, 0))
            for k in range(8):
                srcs.append((x, hv * 1024 + k * 128, 128, 16 + k * 128, 1 + k // 4))
            for (src, r0, nr, c0, ti) in srcs:
                for q in range(4):
                    t = trot.next()
                    DMA("sp", tmps[t][0:nr, :], src[r0:r0 + nr, q * 512:(q + 1) * 512], [], [TM(t)], ("tmpld", t))
                    for cc in range(4):
                        c = q * 4 + cc
                        b = brot.next()
                        TRANSPOSE(banks[b][:, 0:nr], tmps[t][0:nr, cc * 128:(cc + 1) * 128], ident[0:nr, 0:nr],
                                  [TM(t), "cstf"], [PB(b)])
                        ACT(hf[:, c, c0:c0 + nr], banks[b][:, 0:nr], AF.Copy, [PB(b)], [HF(c, ti)])
                        COPY("dve", hb[:, c, c0:c0 + nr], hf[:, c, c0:c0 + nr], [HF(c, ti)], [HB(c, ti)])

        outcells = []

        def store_out(hv):
            for k in range(8):
                ti = 1 + k // 4
                c0 = 16 + k * 128
                for q in range(4):
                    t = trot.next()
                    for cc in range(4):
                        c = q * 4 + cc
                        b = brot.next()
                        TRANSPOSE(banks[b][:, 0:128], hf[:, c, c0:c0 + 128], ident, [HF(c, ti), "cstf"], [PB(b)])
                        if cc % 2 == 0:
                            ACT(tmps[t][:, cc * 128:(cc + 1) * 128], banks[b][:, 0:128], AF.Copy, [PB(b)], [TM(t)])
                        else:
                            COPY("dve", tmps[t][:, cc * 128:(cc + 1) * 128], banks[b][:, 0:128], [PB(b)], [TM(t)])
                    r0 = hv * 1024 + k * 128
                    cell = ("y", hv, k, q)
                    DMA("sp", y[r0:r0 + 128, q * 512:(q + 1) * 512], tmps[t][:, :], [TM(t)], [cell], ("tmpst", t))
                    outcells.append(cell)

        def layer_norm(L, s, tis):
            eps = LN_EPS / (ALPHA * ALPHA)
            for ti in tis:
                a, bnd = TILES[ti]
                n = bnd - a
                b1 = brot.next()
                b2 = brot.next()
                while b2 == b1:
                    b2 = brot.next()
                for c in range(NCH):
                    t = trot.next()
                    sqb = tmps[t].bitcast(BF16)[:, 0:n]
                    zcb = tmps[t].bitcast(BF16)[:, 512:512 + n]
                    COPY("dve", zcb, hf[:, c, a:bnd], [HF(c, ti)], [TM(t)])
                    ACT(sqb, hf[:, c, a:bnd], AF.Square, [HF(c, ti)], [TM(t)])
                    MM1(banks[b1][:, 0:n], pones_b, zcb, c == 0, c == NCH - 1, [TM(t), "cstb"], [PB(b1)])
                    MM1(banks[b2][:, 0:n], pones_b, sqb, c == 0, c == NCH - 1, [TM(t), "cstb"], [PB(b2)])
                tm, tr, tq = trot.next(), trot.next(), trot.next()
                TS("dve", tmps[tm][:, 0:n], banks[b1][:, 0:n], 1.0 / D, None, ALU.mult, None, [PB(b1)], [TM(tm)])
                TT("dve", tmps[tq][:, 0:n], tmps[tm][:, 0:n], tmps[tm][:, 0:n], ALU.mult, [TM(tm)], [TM(tq)])
                STT("dve", tmps[tr][:, 0:n], banks[b2][:, 0:n], 1.0 / D, tmps[tq][:, 0:n], ALU.mult, ALU.subtract,
                    [PB(b2), TM(tq)], [TM(tr)])
                ACT(tmps[tr][:, 0:n], tmps[tr][:, 0:n], AF.Sqrt, [TM(tr), "epsc"], [TM(tr)], bias=eps_ap)
                S.op("dve", lambda e, o=tmps[tr][:, 0:n]: e.reciprocal(out=o, in_=o), reads=[TM(tr)], writes=[TM(tr)])
                for c in range(NCH):
                    t1 = trot.next()
                    while t1 in (tm, tr):
                        t1 = trot.next()
                    TT("dve", tmps[t1][:, 0:n], hf[:, c, a:bnd], tmps[tm][:, 0:n], ALU.subtract, [HF(c, ti), TM(tm)], [TM(t1)])
                    TT("dve", tmps[t1][:, 0:n], tmps[t1][:, 0:n], tmps[tr][:, 0:n], ALU.mult, [TM(t1), TM(tr)], [TM(t1)])
                    ACT(hf[:, c, a:bnd], tmps[t1][:, 0:n], AF.Identity, [TM(t1), "prm"], [HF(c, ti)],
                        bias=P_LNB(L, s, c), scale=P_LNG(L, s, c))
                    ACT(hb[:, c, a:bnd], tmps[t1][:, 0:n], AF.Identity, [TM(t1), "prm"], [HB(c, ti)],
                        bias=P_LNB(L, s, c), scale=P_LNG(L, s, c))

        def ffn(key, tis):
            uidx = list(plan[key])
            up_units = {}
            out_units = {}
            pos = 0
            for blk in range(12):
                if blk < 11:
                    up_units[2 * blk] = uidx[pos]; pos += 1
                    up_units[2 * blk + 1] = uidx[pos]; pos += 1
                if blk >= 1:
                    out_units[blk - 1] = uidx[pos]; pos += 1
            for blk in range(12):
                if blk < 11:
                    ms = blk % 2
                    for jj in range(2):
                        slot = wget(up_units[2 * blk + jj])
                        wv = wsl[slot].rearrange("p (g c f) -> p g c f", g=2, c=16)
                        for ti in tis:
                            for fcl in range(2):
                                a, bnd = TILES[ti]
                                n = bnd - a
                                bg = brot.next()
                                bu = brot.next()
                                rd = [HB(c, ti) for c in range(NCH)] + [("ws", slot)]
                                mm_group(bg, n, [(wv[:, 0, c, fcl * 128:(fcl + 1) * 128], hb[:, c, a:bnd]) for c in range(NCH)], rd)
                                mm_group(bu, n, [(wv[:, 1, c, fcl * 128:(fcl + 1) * 128], hb[:, c, a:bnd]) for c in range(NCH)], rd)
                                t = trot.next()
                                ACT(tmps[t][:, 0:n], banks[bg][:, 0:n], AF.Silu, [PB(bg)], [TM(t)])
                                mc = ms * 4 + jj * 2 + fcl
                                TT("dve", gbuf[:, mc, a:bnd], tmps[t][:, 0:n], banks[bu][:, 0:n], ALU.mult,
                                   [TM(t), PB(bu)], [("g", mc, ti)])
                if blk >= 1:
                    pb = blk - 1
                    ms = pb % 2
                    slot = wget(out_units[pb])
                    wv = wsl[slot].rearrange("p (k f) -> p k f", k=4)
                    for ti in tis:
                        a, bnd = TILES[ti]
                        n = bnd - a
                        for dc in range(NCH):
                            b = brot.next()
                            mm_group(b, n, [(wv[:, k, dc * 128:(dc + 1) * 128], gbuf[:, ms * 4 + k, a:bnd]) for k in range(4)],
                                     [("g", ms * 4 + k, ti) for k in range(4)] + [("ws", slot)])
                            STT("dve", hf[:, dc, a:bnd], banks[b][:, 0:n], 0.5 / ALPHA, hf[:, dc, a:bnd], ALU.mult, ALU.add,
                                [PB(b), HF(dc, ti)], [HF(dc, ti)])

        def out_proj(slots, tis):
            for ti in tis:
                a, bnd = TILES[ti]
                n = bnd - a
                for dco in range(NCH):
                    b = brot.next()
                    pairs = []
                    for k in range(8):
                        wv = wsl[slots[k // 4]].rearrange("p (k f) -> p k f", k=4)
                        pairs.append((wv[:, k % 4, dco * 128:(dco + 1) * 128], gbuf[:, k, a:bnd]))
                    mm_group(b, n, pairs, [("g", k, ti) for k in range(8)] + [("ws", s_) for s_ in slots])
                    STT("dve", hf[:, dco, a:bnd], banks[b][:, 0:n], 1.0 / ALPHA, hf[:, dco, a:bnd], ALU.mult, ALU.add,
                        [PB(b), HF(dco, ti)], [HF(dco, ti)])

        def conv_mixer(hv, l, key, tis):
            uidx = list(plan[key])
            pos = 0
            for half8 in range(2):
                for dcl in range(8):
                    dc = half8 * 8 + dcl
                    slot = wget(uidx[pos]); pos += 1
                    wv = wsl[slot][:, 0:6144].rearrange("p (g c f) -> p g c f", g=3, c=16)
                    vs = dc % 2
                    ve = vext[vs]
                    VC = ("vext", vs)
                    if hv == 0:
                        MEMSET("pool", ve[:, 0:2], 0.0, [VC])
                    else:
                        COPY("pool", ve[:, 16:18], cstate[:, l, dc, :], [("cstate", l, dc)], [VC])
                    for ti in tis:
                        a, bnd = TILES[ti]
                        n = bnd - a
                        rd = [HB(c, ti) for c in range(NCH)] + [("ws", slot)]
                        bc = brot.next()
                        bu = brot.next()
                        bb = brot.next()
                        for g, bk in ((1, bc), (2, bu), (0, bb)):
                            mm_group(bk, n, [(wv[:, g, c, :], hb[:, c, a:bnd]) for c in range(NCH)], rd)
                        t = trot.next()
                        ACT(tmps[t][:, 0:n], banks[bc][:, 0:n], AF.Copy, [PB(bc)], [TM(t)])
                        TT("dve", ve[:, 2 + a:2 + a + n], tmps[t][:, 0:n], banks[bu][:, 0:n], ALU.mult, [TM(t), PB(bu)], [VC])
                        t2 = trot.next()
                        TS("pool", tmps[t2][:, 0:n], ve[:, 2 + a:2 + a + n], P_CW(l, 2, dc), P_CB(l, dc), ALU.mult, ALU.add,
                           [VC, "prm"], [TM(t2)])
                        STT("dve", tmps[t2][:, 0:n], ve[:, 1 + a:1 + a + n], P_CW(l, 1, dc), tmps[t2][:, 0:n], ALU.mult, ALU.add,
                            [VC, "prm", TM(t2)], [TM(t2)])
                        STT("dve", tmps[t2][:, 0:n], ve[:, a:a + n], P_CW(l, 0, dc), tmps[t2][:, 0:n], ALU.mult, ALU.add,
                            [VC, "prm", TM(t2)], [TM(t2)])
                        TT("dve", gbuf[:, dcl, a:bnd], tmps[t2][:, 0:n], banks[bb][:, 0:n], ALU.mult, [TM(t2), PB(bb)], [("g", dcl, ti)])
                    if hv == 0:
                        COPY("pool", cstate[:, l, dc, :], ve[:, SB:SB + 2], [VC], [("cstate", l, dc)])
                s0 = wget(uidx[pos]); pos += 1
                s1 = wget(uidx[pos], ahead=1); pos += 1
                out_proj([s0, s1], tis)

        vd_cells = {}

        def kv_proj(hv, key, tis):
            uidx = list(plan[key])
            tok_base = 0 if hv == 0 else 1024
            for g in range(4):
                slot = wget(uidx[g])
                wv = wsl[slot].rearrange("p (c f) -> p c f", c=16)
                for hl in range(4):
                    h = 4 * g + hl
                    for ti in tis:
                        a, bnd = TILES[ti]
                        n = bnd - a
                        b = brot.next()
                        mm_group(b, n, [(wv[:, c, hl * 128:(hl + 1) * 128], hb[:, c, a:bnd]) for c in range(NCH)],
                                 [HB(c, ti) for c in range(NCH)] + [("ws", slot)])
                        t = trot.next()
                        tb = tmps[t].bitcast(BF16)
                        ACT(tb[:, 0:n], banks[b][:, 0:n], AF.Copy, [PB(b)], [TM(t)])
                        DMA("sp", Kd[h, :, tok_base + a:tok_base + a + n], tb[:, 0:n], [TM(t)], [("Kd", h, hv, ti)], ("tmpst", t))
            blocks = []
            for ti in tis:
                a, bnd = TILES[ti]
                for s0 in range(a, bnd, 128):
                    blocks.append((ti, s0, min(128, bnd - s0)))
            for g in range(4):
                slot = wget(uidx[4 + g])
                wv = wsl[slot].rearrange("p (c f) -> p c f", c=16)
                for (ti, s0, nt) in blocks:
                    b = brot.next()
                    mm_group(b, 512, [(hb[:, c, s0:s0 + nt], wv[:, c, :]) for c in range(NCH)],
                             [HB(c, ti) for c in range(NCH)] + [("ws", slot)], M=nt)
                    t = trot.next()
                    tb = tmps[t].bitcast(BF16)
                    COPY("dve", tb[0:nt, 0:512], banks[b][0:nt, 0:512], [PB(b)], [TM(t)])
                    gt = tok_base + s0
                    cell = ("Vd", g, hv, s0)
                    DMA("sp", Vd[4 * g:4 * g + 4, gt:gt + nt, :].rearrange("h t d -> t h d"),
                        tb[0:nt, 0:512].rearrange("t (h d) -> t h d", h=4), [TM(t)], [cell], ("tmpst", t))
                    vd_cells.setdefault(hv, []).append((g, cell))

        live_ob = set()

        def free_bank(excl=()):
            b = brot.next()
            while b in live_ob or b in excl:
                b = brot.next()
            return b

        def attn_jobs(hv, hl, ks, ti, hc, li):
            qi = ti - 1 + 2 * hv
            kblocks = [("diag", 4 * qi + m, m) for m in (3, 2, 1, 0)]
            kblocks += [("full", kb, 0) for kb in range(4 * qi - 1, -1, -1)]
            kblocks.append(("meta", -1, 0))
            jobs = []
            shared = {}
            for i, (kind, kb, m) in enumerate(kblocks):
                jobs.append(dict(kind=kind, kb=kb, m=m, hl=hl, ks=ks, ti=ti, hc=hc, li=li, first=(i == 0),
                                 last=(i == len(kblocks) - 1), sh=shared))
            return jobs

        def job_geom(J):
            a, bnd = TILES[J["ti"]]
            if J["kind"] == "meta":
                k0, kp, vb = 0, 16, 0
            else:
                k0, kp, vb = 16 + 128 * J["kb"], 128, 1 + J["kb"]
            c0 = 128 * J["m"] if J["kind"] == "diag" else 0
            n = 512 - c0
            qsl = qbuf[:, J["hl"], a + c0:bnd]
            ksl = kbuf[J["ks"]][:, k0:k0 + kp]
            return a, bnd, k0, kp, vb, c0, n, qsl, ksl

        def stage_A(J):
            a, bnd, k0, kp, vb, c0, n, qsl, ksl = job_geom(J)
            ls = lsums[J["li"]]
            if J["first"]:
                ob = free_bank()
                live_ob.add(ob)
                J["sh"]["ob"] = ob
                MM1(banks[ob][:, 0:512], zb, hb[:, 0, 16:528], True, False, ["zb"], [PB(ob)])
                MEMSET("dve", ls, 0.0, [("lsum", J["li"])])
            zbk = free_bank()
            mm_group(zbk, n, [(ksl, qsl)], [("kbuf", J["ks"]), ("q", J["hl"], J["ti"])], M=kp)
            tE = trot.next()
            J["tE"] = tE
            E = tmps[tE][0:kp, 0:n]
            ACT(E, banks[zbk][0:kp, 0:n], AF.Exp, [PB(zbk)], [TM(tE)])
            ACT(E, E, AF.Ln, [TM(tE)], [TM(tE)], bias=1.0)

        def stage_B1(J):
            a, bnd, k0, kp, vb, c0, n, qsl, ksl = job_geom(J)
            tL = trot.next()
            J["tL"] = tL
            E = tmps[J["tE"]][0:kp, 0:n]
            LA = tmps[tL].bitcast(BF16)
            COPY("dve", LA[0:kp, 0:n], E, [TM(J["tE"])], [TM(tL)])
            if J["kind"] == "diag":
                TT("dve", LA[:, 0:128], LA[:, 0:128], tri_b, ALU.mult, [TM(tL), "cstb"], [TM(tL)])

        def stage_B2(J):
            a, bnd, k0, kp, vb, c0, n, qsl, ksl = job_geom(J)
            ls = lsums[J["li"]]
            LC = ("lsum", J["li"])
            SPb = tmps[J["tL"]].bitcast(BF16)[0:kp, 0:n]
            xb = free_bank()
            J["xb"] = xb
            mm_seq(xb, n, [(ksl, qsl), (U_b[0:kp, 0:kp], SPb)], True, False,
                   [("kbuf", J["ks"]), ("q", J["hl"], J["ti"]), TM(J["tL"]), "cstb"], M=kp)
            mm_seq(xb, n, [(ones_b[:, 0:kp], ls[:, c0:512])], False, True, [LC, "cstb"], M=kp)

        def stage_C1(J):
            a, bnd, k0, kp, vb, c0, n, qsl, ksl = job_geom(J)
            ls = lsums[J["li"]]
            LC = ("lsum", J["li"])
            SPb = tmps[J["tL"]].bitcast(BF16)[0:kp, 0:n]
            if J["kind"] != "meta":
                TT("dve", ls[:, c0:512], ls[:, c0:512], SPb, ALU.add, [TM(J["tL"]), LC], [LC])

        def stage_C2(J):
            a, bnd, k0, kp, vb, c0, n, qsl, ksl = job_geom(J)
            tE, tL, xb = J["tE"], J["tL"], J["xb"]
            E = tmps[tE][0:kp, 0:n]
            LA = tmps[tL].bitcast(BF16)
            Ab = LA[0:kp, 512:512 + n]
            TT("dve", E, banks[xb][0:kp, 0:n], E, ALU.subtract, [PB(xb), TM(tE)], [TM(tE)])
            ACT(Ab, E, AF.Exp, [TM(tE)], [TM(tL)])
            if J["kind"] == "diag":
                TT("dve", LA[:, 512:640], LA[:, 512:640], tri_b, ALU.mult, [TM(tL), "cstb"], [TM(tL)])

        def stage_D(J):
            a, bnd, k0, kp, vb, c0, n, qsl, ksl = job_geom(J)
            ob = J["sh"]["ob"]
            LA = tmps[J["tL"]].bitcast(BF16)
            Ab = LA[0:kp, 512:512 + n]
            MM1(banks[ob][:, c0:512], vbuf[0][0:kp, vb, :], Ab, False, J["last"], [TM(J["tL"]), ("vbuf", 0)], [PB(ob)])
            if J["last"]:
                ACT(gbuf[:, J["hc"], a:bnd], banks[ob][:, 0:512], AF.Copy, [PB(ob)], [("g", J["hc"], J["ti"])])
                live_ob.discard(ob)

        def run_pipeline(jobs, preA, preD):
            nj = len(jobs)
            for t in range(nj + 3):
                if 0 <= t - 2 < nj:
                    stage_C1(jobs[t - 2])
                if 0 <= t - 1 < nj:
                    stage_B1(jobs[t - 1])
                if t < nj:
                    if t in preA:
                        preA[t]()
                    stage_A(jobs[t])
                if 0 <= t - 1 < nj:
                    stage_B2(jobs[t - 1])
                if 0 <= t - 2 < nj:
                    stage_C2(jobs[t - 2])
                if 0 <= t - 3 < nj:
                    if t - 3 in preD:
                        preD[t - 3]()
                    stage_D(jobs[t - 3])

        def attn_mixer(hv, j, key):
            uidx = list(plan[key])
            pos = 0
            tis = [1, 2]
            tok_base = 0 if hv == 0 else 1024
            nkeys = tok_base + SB
            nblk = (nkeys - 16) // 128
            for half8 in range(2):
                for g2 in range(2):
                    g = half8 * 2 + g2
                    slot = wget(uidx[pos]); pos += 1
                    wv = wsl[slot].rearrange("p (c f) -> p c f", c=16)
                    for ti in tis:
                        for hl in range(4):
                            a, bnd = TILES[ti]
                            b = brot.next()
                            mm_group(b, 512, [(wv[:, c, hl * 128:(hl + 1) * 128], hb[:, c, a:bnd]) for c in range(NCH)],
                                     [HB(c, ti) for c in range(NCH)] + [("ws", slot)])
                            ACT(qbuf[:, hl, a:bnd], banks[b][:, 0:512], AF.Identity, [PB(b)], [("q", hl, ti)], scale=SCALE)
                    jobs = []
                    preA = {}
                    preC = {}
                    for hl in range(4):
                        h = 4 * g + hl
                        ks = h % 2
                        kcells = []
                        for hh in range(hv + 1):
                            for ti_ in ((0, 1, 2) if hh == 0 else (1, 2)):
                                kcells.append(("Kd", h, hh, ti_))
                        vcells = [c_ for hh in range(hv + 1) for (gg, c_) in vd_cells[hh] if gg == h // 4]

                        def kload(h=h, ks=ks, kcells=kcells):
                            DMA("sp", kbuf[ks][:, 0:nkeys], Kd[h, :, 0:nkeys], kcells, [("kbuf", ks)], ("kbuf", ks))

                        def vload(h=h, vcells=vcells):
                            def vfn(e):
                                return [e.dma_start(out=vbuf[0][0:16, 0, :], in_=Vd[h, 0:16, :]),
                                        e.dma_start(out=vbuf[0][:, 1:1 + nblk, :],
                                                    in_=Vd[h, 16:nkeys, :].rearrange("(b p) d -> p b d", p=128))]
                            S.op("sp", vfn, reads=vcells, writes=[("vbuf", 0)], dma=("vbuf", 0), ndma=2)
                        preA[len(jobs)] = kload
                        preC[len(jobs)] = vload
                        for ti in tis:
                            jobs += attn_jobs(hv, hl, ks, ti, g2 * 4 + hl, lrot.next())
                    run_pipeline(jobs, preA, preC)
                s0 = wget(uidx[pos]); pos += 1
                s1 = wget(uidx[pos], ahead=1); pos += 1
                out_proj([s0, s1], tis)

        for hv in halves:
            load_state(hv)
            for L in range(nlayers):
                tis = [0, 1, 2] if (hv == 0 and L < NCONV) else [1, 2]
                ffn((hv, L, "f0"), tis)
                layer_norm(L, 0, tis)
                if L < NCONV:
                    conv_mixer(hv, L, (hv, L, "mix"), tis)
                else:
                    attn_mixer(hv, L - NCONV, (hv, L, "mix"))
                layer_norm(L, 1, tis)
                ffn((hv, L, "f1"), tis)
                layer_norm(L, 2, tis)
                if L == NCONV - 1:
                    kv_proj(hv, (hv, L, "kv"), tis)
            store_out(hv)
        S.final_wait("sp", outcells)

        with nc.Block() as block:
            S.emit(block)
        stats = dict(nops=S.nops, ecnt=dict(S.ecnt), ndsem=len(S.dsem), words=off[0],
                     per_eng={e: len(v) for e, v in S.prog.items()})
    return nc, stats


def host_consts():
    ident = np.eye(128, dtype=np.float32)
    ones = np.ones((128, 128), np.float32)
    j = np.arange(128)[:, None]
    s = np.arange(128)[None, :]
    U = (j > s).astype(np.float32)
    tri = (s > j).astype(np.float32)
    return np.concatenate([ident, ones, -U, tri, -ones, ones], axis=1)


def make_in_maps(x, meta_tokens, ln_gain, ln_bias, ffn_w_in, ffn_w_out, conv_w_in, conv_w, conv_b,
                 conv_w_out, sb_w_q, sb_w_kv, sb_w_o, ncores=8):
    f = lambda a: np.ascontiguousarray(np.asarray(a, dtype=np.float32))
    prm = np.concatenate([f(ln_gain).reshape(192, 128), f(ln_bias).reshape(192, 128),
                          f(conv_w).reshape(96, 128), f(conv_b).reshape(32, 128)], axis=0)
    shared = {
        "meta": f(meta_tokens), "prm": np.ascontiguousarray(prm), "cst": host_consts(),
        "ffn_w_in": f(ffn_w_in).reshape(DEPTH * 2, D, 2 * DFF), "ffn_w_out": f(ffn_w_out).reshape(DEPTH * 2, DFF, D),
        "conv_w_in": f(conv_w_in), "conv_w_out": f(conv_w_out), "sb_w_q": f(sb_w_q), "sb_w_kv": f(sb_w_kv),
        "sb_w_o": f(sb_w_o),
    }
    xs = f(x)
    return [dict(shared, x=xs[b]) for b in range(ncores)]


def kernel(x, meta_tokens, ln_gain, ln_bias, ffn_w_in, ffn_w_out, conv_w_in, conv_w, conv_b,
           conv_w_out, sb_w_q, sb_w_kv, sb_w_o):
    nc, _ = build_program()
    in_maps = make_in_maps(x, meta_tokens, ln_gain, ln_bias, ffn_w_in, ffn_w_out, conv_w_in, conv_w, conv_b,
                           conv_w_out, sb_w_q, sb_w_kv, sb_w_o)
    res = run_bass_kernel_spmd(nc, in_maps, core_ids=list(range(8)))
    return np.stack([r["y"] for r in res.results], axis=0).astype(np.float32)
```

```python
import numpy as np
from contextlib import ExitStack
import concourse.bass as bass
import concourse.mybir as mybir
from concourse.bass_utils import run_bass_kernel_spmd

F32 = mybir.dt.float32
BF16 = mybir.dt.bfloat16
AF = mybir.ActivationFunctionType
ALU = mybir.AluOpType

D = 2048
NCH = 16
SEQ = 2048
NMETA = 16
T = SEQ + NMETA
DEPTH = 4
NCONV = 2
DFF = 5632
NFC = DFF // 128
NH = 16
HD = 128
ALPHA = (2 * DEPTH) ** 0.25
LN_EPS = 1e-5
SCALE = HD ** -0.5
SB = 1040
TILES = [(0, 16), (16, 528), (528, 1040)]
WSLOT = 8192
NWS = 3


class Sched:
    ENG = ("pe", "act", "dve", "pool", "sp")

    def __init__(self, nc, es):
        self.nc = nc
        self.es = es
        self.prog = {e: [] for e in self.ENG}
        self.esem = {e: es.enter_context(nc.semaphore("s_" + e)) for e in ("pe", "act", "dve", "pool")}
        self.ecnt = {e: 0 for e in self.esem}
        self.dsem = {}
        self.dcnt = {}
        self.known = {e: {} for e in self.ENG}
        self.cw = {}
        self.cr = {}
        self.nops = 0

    def _dma_sem(self, key):
        if key not in self.dsem:
            self.dsem[key] = self.es.enter_context(self.nc.semaphore("d_%d" % len(self.dsem)))
            self.dcnt[key] = 0
        return self.dsem[key]

    def op(self, eng, fn, reads=(), writes=(), dma=None, ndma=1):
        need = {}

        def add(tokd):
            for sid, (sem, val) in tokd.items():
                if need.get(sid, (None, 0))[1] < val:
                    need[sid] = (sem, val)

        for c in reads:
            add(self.cw.get(c, {}))
        for c in writes:
            add(self.cw.get(c, {}))
            add(self.cr.get(c, {}))
        waits = []
        for sid, (sem, val) in need.items():
            if eng == "pe" and dma is None and sem is self.esem["pe"]:
                continue
            if self.known[eng].get(sid, 0) < val:
                self.known[eng][sid] = val
                waits.append((sem, val))
        if dma is not None:
            sem = self._dma_sem(dma)
            self.dcnt[dma] += 16 * ndma
            tok = (sem, self.dcnt[dma])
            inc = 16
        else:
            sem = self.esem[eng]
            self.ecnt[eng] += 1
            tok = (sem, self.ecnt[eng])
            inc = 1
        sid = id(sem)
        self.prog[eng].append((waits, fn, sem, inc))
        for c in reads:
            d = self.cr.setdefault(c, {})
            d[sid] = tok
        for c in writes:
            self.cw[c] = {sid: tok}
            self.cr[c] = {}
        self.nops += 1

    def final_wait(self, eng, cells):
        need = {}
        for c in cells:
            for sid, (sem, val) in self.cw.get(c, {}).items():
                if need.get(sid, (None, 0))[1] < val:
                    need[sid] = (sem, val)
        self.prog[eng].append((list(need.values()), None, None, 0))

    def emit(self, block):
        nc = self.nc

        def run(e, items):
            for waits, fn, sem, inc in items:
                for wsem, val in waits:
                    e.wait_ge(wsem, val)
                if fn is None:
                    continue
                r = fn(e)
                if isinstance(r, (list, tuple)):
                    for ins in r:
                        ins.then_inc(sem, inc)
                else:
                    r.then_inc(sem, inc)

        @block.tensor
        def _(e):
            run(e, self.prog["pe"])

        @block.scalar
        def _(e):
            run(e, self.prog["act"])

        @block.vector
        def _(e):
            run(e, self.prog["dve"])

        @block.gpsimd
        def _(e):
            run(e, self.prog["pool"])

        @block.sync
        def _(e):
            run(e, self.prog["sp"])


class Rot:
    def __init__(self, items):
        self.items = list(items)
        self.i = 0

    def next(self):
        r = self.items[self.i % len(self.items)]
        self.i += 1
        return r


def build_program(nlayers=DEPTH, halves=(0, 1)):
    nc = bass.Bass("TRN2", target_bir_lowering=False)
    es = ExitStack()
    with es:
        dr = {}

        def din(name, shape):
            dr[name] = nc.dram_tensor(name, list(shape), F32, kind="ExternalInput").ap()
            return dr[name]

        x = din("x", [SEQ, D])
        meta = din("meta", [NMETA, D])
        prm_d = din("prm", [512, 128])
        cst_d = din("cst", [128, 6 * 128])
        w_ffn_in = din("ffn_w_in", [DEPTH * 2, D, 2 * DFF])
        w_ffn_out = din("ffn_w_out", [DEPTH * 2, DFF, D])
        w_cin = din("conv_w_in", [NCONV, D, 3 * D])
        w_cout = din("conv_w_out", [NCONV, D, D])
        w_q = din("sb_w_q", [2, D, D])
        w_kv = din("sb_w_kv", [D, 2 * D])
        w_o = din("sb_w_o", [2, D, D])
        y = nc.dram_tensor("y", [SEQ, D], F32, kind="ExternalOutput").ap()
        Kd = nc.dram_tensor("Kd", [NH, HD, T], BF16).ap()
        Vd = nc.dram_tensor("Vd", [NH, T, HD], BF16).ap()

        AW = 52600
        arena = es.enter_context(nc.sbuf_tensor("arena", [128, AW], F32))
        off = [0]

        def carve(nwords):
            a = off[0]
            off[0] += (nwords + 7) // 8 * 8
            assert off[0] <= AW, off[0]
            return arena[:, a:a + nwords]

        hf = carve(NCH * SB).rearrange("p (c t) -> p c t", c=NCH)
        hb = carve(NCH * SB // 2).bitcast(BF16).rearrange("p (c t) -> p c t", c=NCH)
        wsl = [carve(WSLOT // 2).bitcast(BF16) for _ in range(NWS)]
        gbuf = carve(8 * SB // 2).bitcast(BF16).rearrange("p (c t) -> p c t", c=8)
        NTMP = 8
        tmps = [carve(512) for _ in range(NTMP)]
        prm = carve(512)
        cstf = carve(2 * 128)
        cstb = carve(4 * 64).bitcast(BF16)
        zb = carve(64).bitcast(BF16)
        epsc = carve(8)
        eps_ap = epsc[:, 0:1]
        cstate = carve(NCONV * NCH * 2).rearrange("p (l c k) -> p l c k", l=NCONV, c=NCH)
        ovl0 = off[0]
        vext = [carve(SB + 2) for _ in range(2)]
        off[0] = ovl0
        qbuf = carve(4 * SB // 2).bitcast(BF16).rearrange("p (h t) -> p h t", h=4)
        kbuf = [carve(T // 2).bitcast(BF16) for _ in range(2)]
        vbuf = [carve(17 * 128 // 2).bitcast(BF16).rearrange("p (b d) -> p b d", b=17)]
        lsums = [carve(256).bitcast(BF16) for _ in range(2)]
        ident = cstf[:, 0:128]
        ones_f = cstf[:, 128:256]
        U_b = cstb[:, 0:128]
        tri_b = cstb[:, 128:256]
        ones_b = cstb[:, 256:384]
        pones_b = cstb[:, 384:512]

        banks = [es.enter_context(nc.psum_tensor("bank%d" % i, [128, 512], F32)) for i in range(8)]

        S = Sched(nc, es)
        brot = Rot(list(range(8)))
        trot = Rot(list(range(NTMP)))
        lrot = Rot([0, 1])

        def PB(i):
            return ("bank", i)

        def TM(i):
            return ("tmp", i)

        units = []
        wstate = {"issued": 0}

        def colblock(wm, f0, w):
            return wm[:, f0:f0 + w].rearrange("(c p) f -> p c f", p=128)

        def rowblock(wm, r0, nr):
            return wm[r0 * 128:(r0 + nr) * 128, :].rearrange("(c p) f -> p c f", p=128)

        def issue_unit(i):
            parts = units[i]
            slot = i % NWS

            def fn(e, parts=parts, slot=slot):
                res = []
                for lo, c, w, src in parts:
                    dst = wsl[slot][:, lo:lo + c * w].rearrange("p (c f) -> p c f", c=c)
                    res.append(e.dma_start(out=dst, in_=src))
                return res

            S.op("pool", fn, reads=(), writes=[("ws", slot)], dma=("ws", slot), ndma=len(parts))

        def wget(i, ahead=NWS - 1):
            while wstate["issued"] < min(len(units), i + ahead + 1):
                issue_unit(wstate["issued"])
                wstate["issued"] += 1
            return i % NWS

        plan = {}

        def plan_units(key, lst):
            plan[key] = list(range(len(units), len(units) + len(lst)))
            units.extend(lst)

        def ffn_units(fi):
            lst = []
            wi, wo = w_ffn_in[fi], w_ffn_out[fi]
            win = lambda j: [(0, 16, 256, colblock(wi, j * 256, 256)),
                             (4096, 16, 256, colblock(wi, DFF + j * 256, 256))]
            wout = lambda b: [(0, 4, D, rowblock(wo, b * 4, 4))]
            order = []
            for blk in range(12):
                if blk < 11:
                    order.append(win(2 * blk))
                    order.append(win(2 * blk + 1))
                if blk >= 1:
                    order.append(wout(blk - 1))
            return order

        def conv_units(l):
            lst = []
            for half8 in range(2):
                for dc in range(half8 * 8, half8 * 8 + 8):
                    lst.append([(k * 2048, 16, 128, colblock(w_cin[l], k * D + dc * 128, 128)) for k in range(3)])
                for r in range(2):
                    lst.append([(0, 4, D, rowblock(w_cout[l], half8 * 8 + r * 4, 4))])
            return lst

        def attn_units(j):
            lst = []
            for half8 in range(2):
                for g in range(2):
                    lst.append([(0, 16, 512, colblock(w_q[j], (half8 * 2 + g) * 512, 512))])
                for r in range(2):
                    lst.append([(0, 4, D, rowblock(w_o[j], half8 * 8 + r * 4, 4))])
            return lst

        def kv_units():
            return [[(0, 16, 512, colblock(w_kv, g * 512, 512))] for g in range(8)]

        for hv in halves:
            for L in range(nlayers):
                plan_units((hv, L, "f0"), ffn_units(2 * L))
                if L < NCONV:
                    plan_units((hv, L, "mix"), conv_units(L))
                else:
                    plan_units((hv, L, "mix"), attn_units(L - NCONV))
                plan_units((hv, L, "f1"), ffn_units(2 * L + 1))
                if L == NCONV - 1:
                    plan_units((hv, L, "kv"), kv_units())

        def HF(c, ti):
            return ("hf", c, ti)

        def HB(c, ti):
            return ("hb", c, ti)

        def mm_group(bank, n, pairs, reads, M=128):
            def fn(e):
                last = None
                k = len(pairs)
                for i, (l, r) in enumerate(pairs):
                    last = e.matmul(banks[bank][0:M, 0:n], lhsT=l, rhs=r, start=(i == 0), stop=(i == k - 1))
                return last
            S.op("pe", fn, reads=reads, writes=[PB(bank)])


        def mm_seq(bank, n, pairs, start_first, stop_last, reads, M=128):
            def fn(e):
                last = None
                k = len(pairs)
                for i, (l, r) in enumerate(pairs):
                    last = e.matmul(banks[bank][0:M, 0:n], lhsT=l, rhs=r, start=(start_first and i == 0),
                                    stop=(stop_last and i == k - 1), skip_group_check=True)
                return last
            S.op("pe", fn, reads=reads, writes=[PB(bank)])

        def ACT(out, in_, func, reads, writes, **kw):
            S.op("act", lambda e: e.activation(out=out, in_=in_, func=func, **kw), reads=reads, writes=writes)

        def TT(eng, out, in0, in1, op, reads, writes):
            S.op(eng, lambda e: e.tensor_tensor(out=out, in0=in0, in1=in1, op=op), reads=reads, writes=writes)

        def TS(eng, out, in0, s1, s2, op0, op1, reads, writes):
            if op1 is None:
                S.op(eng, lambda e: e.tensor_scalar(out=out, in0=in0, scalar1=s1, scalar2=None, op0=op0), reads=reads, writes=writes)
            else:
                S.op(eng, lambda e: e.tensor_scalar(out=out, in0=in0, scalar1=s1, scalar2=s2, op0=op0, op1=op1), reads=reads, writes=writes)

        def STT(eng, out, in0, scalar, in1, op0, op1, reads, writes):
            S.op(eng, lambda e: e.scalar_tensor_tensor(out=out, in0=in0, scalar=scalar, in1=in1, op0=op0, op1=op1),
                 reads=reads, writes=writes)

        def COPY(eng, out, in_, reads, writes):
            S.op(eng, lambda e: e.tensor_copy(out=out, in_=in_), reads=reads, writes=writes)

        def MEMSET(eng, ap, val, writes):
            S.op(eng, lambda e: e.memset(ap, val), writes=writes)

        def DMA(eng, out, in_, reads, writes, key):
            S.op(eng, lambda e: e.dma_start(out=out, in_=in_), reads=reads, writes=writes, dma=key)

        def TRANSPOSE(out, in_, idn, reads, writes):
            S.op("pe", lambda e: e.transpose(out, in_, idn), reads=reads, writes=writes)

        def MM1(out, lhsT, rhs, start, stop, reads, writes):
            S.op("pe", lambda e: e.matmul(out, lhsT=lhsT, rhs=rhs, start=start, stop=stop, skip_group_check=True),
                 reads=reads, writes=writes)

        DMA("sp", cstf, cst_d[:, 0:256], [], ["cstf"], "cstf")
        DMA("pool", cstb, cst_d[:, 256:768], [], ["cstb"], "cstb")
        MEMSET("pool", zb, 0.0, ["zb"])
        MEMSET("pool", epsc, LN_EPS / (ALPHA * ALPHA), ["epsc"])
        for r in range(4):
            t = trot.next()
            DMA("sp", tmps[t][:, 0:128], prm_d[r * 128:(r + 1) * 128, :], [], [TM(t)], ("tmpld", t))
            b = brot.next()
            TRANSPOSE(banks[b][:, 0:128], tmps[t][:, 0:128], ident, [TM(t), "cstf"], [PB(b)])
            COPY("dve", prm[:, r * 128:(r + 1) * 128], banks[b][:, 0:128], [PB(b)], ["prm"])

        def P_LNG(L, s, c):
            i = (L * 3 + s) * 16 + c
            return prm[:, i:i + 1]

        def P_LNB(L, s, c):
            i = 192 + (L * 3 + s) * 16 + c
            return prm[:, i:i + 1]

        def P_CW(l, tap, c):
            i = 384 + (l * 3 + tap) * 16 + c
            return prm[:, i:i + 1]

        def P_CB(l, c):
            i = 480 + l * 16 + c
            return prm[:, i:i + 1]

        def load_state(hv):
            if hv == 0:
                for q in range(4):
                    t = trot.next()
                    DMA("sp", tmps[t][0:16, :], meta[0:16, q * 512:(q + 1) * 512], [], [TM(t)], ("tmpld", t))
                    for cc in range(4):
                        c = q * 4 + cc
                        b = brot.next()
                        TRANSPOSE(banks[b][:, 0:16], tmps[t][0:16, cc * 128:(cc + 1) * 128], ident[0:16, 0:16],
                                  [TM(t), "cstf"], [PB(b)])
                        ACT(hf[:, c, 0:16], banks[b][:, 0:16], AF.Copy, [PB(b)], [HF(c, 0)])
                        COPY("dve", hb[:, c, 0:16], hf[:, c, 0:16], [HF(c, 0)], [HB(c, 0)])
            for ti in (1, 2):
                a, bnd = TILES[ti]
                for q in range(4):
                    ts = []
                    for kb in range(4):
                        t = trot.next()
                        ts.append(t)
                        r0 = hv * 1024 + (ti - 1) * 512 + kb * 128
                        DMA("sp", tmps[t][:, :], x[r0:r0 + 128, q * 512:(q + 1) * 512], [], [TM(t)], ("tmpld", t))
                    for cc in range(4):
                        c = q * 4 + cc
                        b = brot.next()
                        for kb in range(4):
                            TRANSPOSE(banks[b][:, kb * 128:(kb + 1) * 128], tmps[ts[kb]][:, cc * 128:(cc + 1) * 128], ident,
                                      [TM(ts[kb]), "cstf"], [PB(b)])
                        ACT(hf[:, c, a:bnd], banks[b][:, 0:512], AF.Copy, [PB(b)], [HF(c, ti)])
                        COPY("dve", hb[:, c, a:bnd], hf[:, c, a:bnd], [HF(c, ti)], [HB(c, ti)])

        outcells = []

        def store_out(hv):
            for k in range(8):
                ti = 1 + k // 4
                c0 = 16 + k * 128
                for q in range(4):
                    t = trot.next()
                    b = brot.next()
                    for cc in range(4):
                        c = q * 4 + cc
                        TRANSPOSE(banks[b][:, cc * 128:(cc + 1) * 128], hf[:, c, c0:c0 + 128], ident, [HF(c, ti), "cstf"], [PB(b)])
                    if (k * 4 + q) % 2 == 0:
                        ACT(tmps[t][:, :], banks[b][:, 0:512], AF.Copy, [PB(b)], [TM(t)])
                    else:
                        COPY("dve", tmps[t][:, :], banks[b][:, 0:512], [PB(b)], [TM(t)])
                    r0 = hv * 1024 + k * 128
                    cell = ("y", hv, k, q)
                    DMA("sp", y[r0:r0 + 128, q * 512:(q + 1) * 512], tmps[t][:, :], [TM(t)], [cell], ("tmpst", t))
                    outcells.append(cell)

        def ln_stats(ti):
            a, bnd = TILES[ti]
            n = bnd - a
            b1 = brot.next()
            b2 = brot.next()
            while b2 == b1:
                b2 = brot.next()
            for c in range(NCH):
                t = trot.next()
                sqb = tmps[t].bitcast(BF16)[:, 0:n]
                zcb = tmps[t].bitcast(BF16)[:, 512:512 + n]
                COPY("dve", zcb, hf[:, c, a:bnd], [HF(c, ti)], [TM(t)])
                ACT(sqb, hf[:, c, a:bnd], AF.Square, [HF(c, ti)], [TM(t)])
                MM1(banks[b1][:, 0:n], pones_b, zcb, c == 0, c == NCH - 1, [TM(t), "cstb"], [PB(b1)])
                MM1(banks[b2][:, 0:n], pones_b, sqb, c == 0, c == NCH - 1, [TM(t), "cstb"], [PB(b2)])
            return dict(ti=ti, b1=b1, b2=b2)

        def ln_fin(cx):
            a, bnd = TILES[cx["ti"]]
            n = bnd - a
            b1, b2 = cx["b1"], cx["b2"]
            tm, tr, tq = trot.next(), trot.next(), trot.next()
            TS("dve", tmps[tm][:, 0:n], banks[b1][:, 0:n], 1.0 / D, None, ALU.mult, None, [PB(b1)], [TM(tm)])
            TT("dve", tmps[tq][:, 0:n], tmps[tm][:, 0:n], tmps[tm][:, 0:n], ALU.mult, [TM(tm)], [TM(tq)])
            STT("dve", tmps[tr][:, 0:n], banks[b2][:, 0:n], 1.0 / D, tmps[tq][:, 0:n], ALU.mult, ALU.subtract,
                [PB(b2), TM(tq)], [TM(tr)])
            ACT(tmps[tr][:, 0:n], tmps[tr][:, 0:n], AF.Sqrt, [TM(tr), "epsc"], [TM(tr)], bias=eps_ap)
            S.op("dve", lambda e, o=tmps[tr][:, 0:n]: e.reciprocal(out=o, in_=o), reads=[TM(tr)], writes=[TM(tr)])
            cx["tm"], cx["tr"] = tm, tr
            ln_live.add(tm)
            ln_live.add(tr)

        def ln_apply(cx, L, s, chunks):
            ti = cx["ti"]
            a, bnd = TILES[ti]
            n = bnd - a
            tm, tr = cx["tm"], cx["tr"]
            for c in chunks:
                t1 = trot.next()
                while t1 in ln_live:
                    t1 = trot.next()
                TT("dve", tmps[t1][:, 0:n], hf[:, c, a:bnd], tmps[tm][:, 0:n], ALU.subtract, [HF(c, ti), TM(tm)], [TM(t1)])
                TT("dve", tmps[t1][:, 0:n], tmps[t1][:, 0:n], tmps[tr][:, 0:n], ALU.mult, [TM(t1), TM(tr)], [TM(t1)])
                ACT(hf[:, c, a:bnd], tmps[t1][:, 0:n], AF.Identity, [TM(t1), "prm"], [HF(c, ti)],
                    bias=P_LNB(L, s, c), scale=P_LNG(L, s, c))
                ACT(hb[:, c, a:bnd], tmps[t1][:, 0:n], AF.Identity, [TM(t1), "prm"], [HB(c, ti)],
                    bias=P_LNB(L, s, c), scale=P_LNG(L, s, c))
            if chunks and chunks[-1] == NCH - 1:
                ln_live.discard(tm)
                ln_live.discard(tr)

        ln_live = set()

        def final_groups_with_ln(tis, emit_group, L, s):
            prev = None
            for ti in tis:
                for dco in range(NCH):
                    emit_group(ti, dco)
                    if prev is not None:
                        if dco == 7:
                            prev.update(ln_stats(prev["ti"]))
                        elif dco == 11:
                            ln_fin(prev)
                        elif dco >= 12:
                            ln_apply(prev, L, s, list(range((dco - 12) * 4, (dco - 11) * 4)))
                prev = dict(ti=ti)
            prev.update(ln_stats(prev["ti"]))
            ln_fin(prev)
            ln_apply(prev, L, s, list(range(NCH)))

        def ffn(key, tis, L, s):
            uidx = list(plan[key])
            up_units = {}
            out_units = {}
            pos = 0
            for blk in range(12):
                if blk < 11:
                    up_units[2 * blk] = uidx[pos]; pos += 1
                    up_units[2 * blk + 1] = uidx[pos]; pos += 1
                if blk >= 1:
                    out_units[blk - 1] = uidx[pos]; pos += 1
            for blk in range(12):
                if blk < 11:
                    ms = blk % 2
                    for jj in range(2):
                        slot = wget(up_units[2 * blk + jj])
                        wv = wsl[slot].rearrange("p (g c f) -> p g c f", g=2, c=16)
                        for ti in tis:
                            for fcl in range(2):
                                a, bnd = TILES[ti]
                                n = bnd - a
                                bg = brot.next()
                                bu = brot.next()
                                rd = [HB(c, ti) for c in range(NCH)] + [("ws", slot)]
                                mm_group(bg, n, [(wv[:, 0, c, fcl * 128:(fcl + 1) * 128], hb[:, c, a:bnd]) for c in range(NCH)], rd)
                                mm_group(bu, n, [(wv[:, 1, c, fcl * 128:(fcl + 1) * 128], hb[:, c, a:bnd]) for c in range(NCH)], rd)
                                t = trot.next()
                                ACT(tmps[t][:, 0:n], banks[bg][:, 0:n], AF.Silu, [PB(bg)], [TM(t)])
                                mc = ms * 4 + jj * 2 + fcl
                                TT("dve", gbuf[:, mc, a:bnd], tmps[t][:, 0:n], banks[bu][:, 0:n], ALU.mult,
                                   [TM(t), PB(bu)], [("g", mc, ti)])
                if blk >= 1:
                    pb = blk - 1
                    ms = pb % 2
                    slot = wget(out_units[pb])
                    wv = wsl[slot].rearrange("p (k f) -> p k f", k=4)

                    def down_group(ti, dc, wv=wv, ms=ms, slot=slot):
                        a, bnd = TILES[ti]
                        n = bnd - a
                        b = brot.next()
                        mm_group(b, n, [(wv[:, k, dc * 128:(dc + 1) * 128], gbuf[:, ms * 4 + k, a:bnd]) for k in range(4)],
                                 [("g", ms * 4 + k, ti) for k in range(4)] + [("ws", slot)])
                        STT("dve", hf[:, dc, a:bnd], banks[b][:, 0:n], 0.5 / ALPHA, hf[:, dc, a:bnd], ALU.mult, ALU.add,
                            [PB(b), HF(dc, ti)], [HF(dc, ti)])
                    if pb < 10:
                        for ti in tis:
                            for dc in range(NCH):
                                down_group(ti, dc)
                    else:
                        final_groups_with_ln(tis, down_group, L, s)

        def out_proj(slots, tis, ln=None):
            def group(ti, dco):
                a, bnd = TILES[ti]
                n = bnd - a
                b = brot.next()
                pairs = []
                for k in range(8):
                    wv = wsl[slots[k // 4]].rearrange("p (k f) -> p k f", k=4)
                    pairs.append((wv[:, k % 4, dco * 128:(dco + 1) * 128], gbuf[:, k, a:bnd]))
                mm_group(b, n, pairs, [("g", k, ti) for k in range(8)] + [("ws", s_) for s_ in slots])
                STT("dve", hf[:, dco, a:bnd], banks[b][:, 0:n], 1.0 / ALPHA, hf[:, dco, a:bnd], ALU.mult, ALU.add,
                    [PB(b), HF(dco, ti)], [HF(dco, ti)])
            if ln is None:
                for ti in tis:
                    for dco in range(NCH):
                        group(ti, dco)
            else:
                final_groups_with_ln(tis, group, ln[0], ln[1])

        def conv_mixer(hv, l, key, tis, L):
            uidx = list(plan[key])
            pos = 0
            for half8 in range(2):
                for dcl in range(8):
                    dc = half8 * 8 + dcl
                    slot = wget(uidx[pos]); pos += 1
                    wv = wsl[slot][:, 0:6144].rearrange("p (g c f) -> p g c f", g=3, c=16)
                    vs = dc % 2
                    ve = vext[vs]
                    VC = ("vext", vs)
                    if hv == 0:
                        MEMSET("pool", ve[:, 0:2], 0.0, [VC])
                    else:
                        COPY("pool", ve[:, 16:18], cstate[:, l, dc, :], [("cstate", l, dc)], [VC])
                    for ti in tis:
                        a, bnd = TILES[ti]
                        n = bnd - a
                        rd = [HB(c, ti) for c in range(NCH)] + [("ws", slot)]
                        bc = brot.next()
                        bu = brot.next()
                        bb = brot.next()
                        for g, bk in ((1, bc), (2, bu), (0, bb)):
                            mm_group(bk, n, [(wv[:, g, c, :], hb[:, c, a:bnd]) for c in range(NCH)], rd)
                        t = trot.next()
                        ACT(tmps[t][:, 0:n], banks[bc][:, 0:n], AF.Copy, [PB(bc)], [TM(t)])
                        TT("dve", ve[:, 2 + a:2 + a + n], tmps[t][:, 0:n], banks[bu][:, 0:n], ALU.mult, [TM(t), PB(bu)], [VC])
                        t2 = trot.next()
                        TS("pool", tmps[t2][:, 0:n], ve[:, 2 + a:2 + a + n], P_CW(l, 2, dc), P_CB(l, dc), ALU.mult, ALU.add,
                           [VC, "prm"], [TM(t2)])
                        STT("dve", tmps[t2][:, 0:n], ve[:, 1 + a:1 + a + n], P_CW(l, 1, dc), tmps[t2][:, 0:n], ALU.mult, ALU.add,
                            [VC, "prm", TM(t2)], [TM(t2)])
                        STT("dve", tmps[t2][:, 0:n], ve[:, a:a + n], P_CW(l, 0, dc), tmps[t2][:, 0:n], ALU.mult, ALU.add,
                            [VC, "prm", TM(t2)], [TM(t2)])
                        TT("dve", gbuf[:, dcl, a:bnd], tmps[t2][:, 0:n], banks[bb][:, 0:n], ALU.mult, [TM(t2), PB(bb)], [("g", dcl, ti)])
                    if hv == 0:
                        COPY("pool", cstate[:, l, dc, :], ve[:, SB:SB + 2], [VC], [("cstate", l, dc)])
                s0 = wget(uidx[pos]); pos += 1
                s1 = wget(uidx[pos], ahead=1); pos += 1
                out_proj([s0, s1], tis, ln=((L, 1) if half8 == 1 else None))

        vd_cells = {}

        def kv_proj(hv, key, tis):
            uidx = list(plan[key])
            tok_base = 0 if hv == 0 else 1024
            for g in range(4):
                slot = wget(uidx[g])
                wv = wsl[slot].rearrange("p (c f) -> p c f", c=16)
                for hl in range(4):
                    h = 4 * g + hl
                    for ti in tis:
                        a, bnd = TILES[ti]
                        n = bnd - a
                        b = brot.next()
                        mm_group(b, n, [(wv[:, c, hl * 128:(hl + 1) * 128], hb[:, c, a:bnd]) for c in range(NCH)],
                                 [HB(c, ti) for c in range(NCH)] + [("ws", slot)])
                        t = trot.next()
                        tb = tmps[t].bitcast(BF16)
                        ACT(tb[:, 0:n], banks[b][:, 0:n], AF.Copy, [PB(b)], [TM(t)])
                        DMA("sp", Kd[h, :, tok_base + a:tok_base + a + n], tb[:, 0:n], [TM(t)], [("Kd", h, hv, ti)], ("tmpst", t))
            blocks = []
            for ti in tis:
                a, bnd = TILES[ti]
                for s0 in range(a, bnd, 128):
                    blocks.append((ti, s0, min(128, bnd - s0)))
            for g in range(4):
                slot = wget(uidx[4 + g])
                wv = wsl[slot].rearrange("p (c f) -> p c f", c=16)
                for (ti, s0, nt) in blocks:
                    b = brot.next()
                    mm_group(b, 512, [(hb[:, c, s0:s0 + nt], wv[:, c, :]) for c in range(NCH)],
                             [HB(c, ti) for c in range(NCH)] + [("ws", slot)], M=nt)
                    t = trot.next()
                    tb = tmps[t].bitcast(BF16)
                    COPY("dve", tb[0:nt, 0:512], banks[b][0:nt, 0:512], [PB(b)], [TM(t)])
                    gt = tok_base + s0
                    cell = ("Vd", g, hv, s0)
                    DMA("sp", Vd[4 * g:4 * g + 4, gt:gt + nt, :].rearrange("h t d -> t h d"),
                        tb[0:nt, 0:512].rearrange("t (h d) -> t h d", h=4), [TM(t)], [cell], ("tmpst", t))
                    vd_cells.setdefault(hv, []).append((g, cell))

        live_ob = set()

        def free_bank(excl=()):
            b = brot.next()
            while b in live_ob or b in excl:
                b = brot.next()
            return b

        def attn_jobs(hv, hl, ks, ti, hc, li):
            qi = ti - 1 + 2 * hv
            kblocks = [("diag", 4 * qi + m, m) for m in (3, 2, 1, 0)]
            kblocks += [("full", kb, 0) for kb in range(4 * qi - 1, -1, -1)]
            kblocks.append(("meta", -1, 0))
            jobs = []
            shared = {}
            for i, (kind, kb, m) in enumerate(kblocks):
                jobs.append(dict(kind=kind, kb=kb, m=m, hl=hl, ks=ks, ti=ti, hc=hc, li=li, first=(i == 0),
                                 last=(i == len(kblocks) - 1), sh=shared))
            return jobs

        def job_geom(J):
            a, bnd = TILES[J["ti"]]
            if J["kind"] == "meta":
                k0, kp, vb = 0, 16, 0
            else:
                k0, kp, vb = 16 + 128 * J["kb"], 128, 1 + J["kb"]
            c0 = 128 * J["m"] if J["kind"] == "diag" else 0
            n = 512 - c0
            qsl = qbuf[:, J["hl"], a + c0:bnd]
            ksl = kbuf[J["ks"]][:, k0:k0 + kp]
            return a, bnd, k0, kp, vb, c0, n, qsl, ksl

        def stage_A(J):
            a, bnd, k0, kp, vb, c0, n, qsl, ksl = job_geom(J)
            ls = lsums[J["li"]]
            if J["first"]:
                ob = free_bank()
                live_ob.add(ob)
                J["sh"]["ob"] = ob
                MM1(banks[ob][:, 0:512], zb, hb[:, 0, 16:528], True, False, ["zb"], [PB(ob)])
                MEMSET("dve", ls, 0.0, [("lsum", J["li"])])
            zbk = free_bank()
            mm_group(zbk, n, [(ksl, qsl)], [("kbuf", J["ks"]), ("q", J["hl"], J["ti"])], M=kp)
            tE = trot.next()
            J["tE"] = tE
            E = tmps[tE][0:kp, 0:n]
            ACT(E, banks[zbk][0:kp, 0:n], AF.Exp, [PB(zbk)], [TM(tE)])
            ACT(E, E, AF.Ln, [TM(tE)], [TM(tE)], bias=1.0)

        def stage_B1(J):
            a, bnd, k0, kp, vb, c0, n, qsl, ksl = job_geom(J)
            tL = trot.next()
            J["tL"] = tL
            E = tmps[J["tE"]][0:kp, 0:n]
            LA = tmps[tL].bitcast(BF16)
            COPY("dve", LA[0:kp, 0:n], E, [TM(J["tE"])], [TM(tL)])
            if J["kind"] == "diag":
                TT("dve", LA[:, 0:128], LA[:, 0:128], tri_b, ALU.mult, [TM(tL), "cstb"], [TM(tL)])

        def stage_B2(J):
            a, bnd, k0, kp, vb, c0, n, qsl, ksl = job_geom(J)
            ls = lsums[J["li"]]
            LC = ("lsum", J["li"])
            SPb = tmps[J["tL"]].bitcast(BF16)[0:kp, 0:n]
            xb = free_bank()
            J["xb"] = xb
            mm_seq(xb, n, [(ksl, qsl), (U_b[0:kp, 0:kp], SPb)], True, False,
                   [("kbuf", J["ks"]), ("q", J["hl"], J["ti"]), TM(J["tL"]), "cstb"], M=kp)
            mm_seq(xb, n, [(ones_b[:, 0:kp], ls[:, c0:512])], False, True, [LC, "cstb"], M=kp)

        def stage_C1(J):
            a, bnd, k0, kp, vb, c0, n, qsl, ksl = job_geom(J)
            ls = lsums[J["li"]]
            LC = ("lsum", J["li"])
            SPb = tmps[J["tL"]].bitcast(BF16)[0:kp, 0:n]
            if J["kind"] != "meta":
                TT("dve", ls[:, c0:512], ls[:, c0:512], SPb, ALU.add, [TM(J["tL"]), LC], [LC])

        def stage_C2(J):
            a, bnd, k0, kp, vb, c0, n, qsl, ksl = job_geom(J)
            tE, tL, xb = J["tE"], J["tL"], J["xb"]
            E = tmps[tE][0:kp, 0:n]
            LA = tmps[tL].bitcast(BF16)
            Ab = LA[0:kp, 512:512 + n]
            TT("dve", E, banks[xb][0:kp, 0:n], E, ALU.subtract, [PB(xb), TM(tE)], [TM(tE)])
            ACT(Ab, E, AF.Exp, [TM(tE)], [TM(tL)])
            if J["kind"] == "diag":
                TT("dve", LA[:, 512:640], LA[:, 512:640], tri_b, ALU.mult, [TM(tL), "cstb"], [TM(tL)])

        def stage_D(J):
            a, bnd, k0, kp, vb, c0, n, qsl, ksl = job_geom(J)
            ob = J["sh"]["ob"]
            LA = tmps[J["tL"]].bitcast(BF16)
            Ab = LA[0:kp, 512:512 + n]
            MM1(banks[ob][:, c0:512], vbuf[0][0:kp, vb, :], Ab, False, J["last"], [TM(J["tL"]), ("vbuf", 0)], [PB(ob)])
            if J["last"]:
                ACT(gbuf[:, J["hc"], a:bnd], banks[ob][:, 0:512], AF.Copy, [PB(ob)], [("g", J["hc"], J["ti"])])
                live_ob.discard(ob)

        def run_pipeline(jobs, preA, preD):
            nj = len(jobs)
            for t in range(nj + 3):
                if 0 <= t - 2 < nj:
                    stage_C1(jobs[t - 2])
                if 0 <= t - 1 < nj:
                    stage_B1(jobs[t - 1])
                if t < nj:
                    if t in preA:
                        preA[t]()
                    stage_A(jobs[t])
                if 0 <= t - 1 < nj:
                    stage_B2(jobs[t - 1])
                if 0 <= t - 2 < nj:
                    stage_C2(jobs[t - 2])
                if 0 <= t - 3 < nj:
                    if t - 3 in preD:
                        preD[t - 3]()
                    stage_D(jobs[t - 3])

        def attn_mixer(hv, j, key, L):
            uidx = list(plan[key])
            pos = 0
            tis = [1, 2]
            tok_base = 0 if hv == 0 else 1024
            nkeys = tok_base + SB
            nblk = (nkeys - 16) // 128
            for half8 in range(2):
                for g2 in range(2):
                    g = half8 * 2 + g2
                    slot = wget(uidx[pos]); pos += 1
                    wv = wsl[slot].rearrange("p (c f) -> p c f", c=16)
                    for ti in tis:
                        for hl in range(4):
                            a, bnd = TILES[ti]
                            b = brot.next()
                            mm_group(b, 512, [(wv[:, c, hl * 128:(hl + 1) * 128], hb[:, c, a:bnd]) for c in range(NCH)],
                                     [HB(c, ti) for c in range(NCH)] + [("ws", slot)])
                            ACT(qbuf[:, hl, a:bnd], banks[b][:, 0:512], AF.Identity, [PB(b)], [("q", hl, ti)], scale=SCALE)
                    jobs = []
                    preA = {}
                    preC = {}
                    for hl in range(4):
                        h = 4 * g + hl
                        ks = h % 2
                        kcells = []
                        for hh in range(hv + 1):
                            for ti_ in ((0, 1, 2) if hh == 0 else (1, 2)):
                                kcells.append(("Kd", h, hh, ti_))
                        vcells = [c_ for hh in range(hv + 1) for (gg, c_) in vd_cells[hh] if gg == h // 4]

                        def kload(h=h, ks=ks, kcells=kcells):
                            DMA("sp", kbuf[ks][:, 0:nkeys], Kd[h, :, 0:nkeys], kcells, [("kbuf", ks)], ("kbuf", ks))

                        def vload(h=h, vcells=vcells):
                            def vfn(e):
                                return [e.dma_start(out=vbuf[0][0:16, 0, :], in_=Vd[h, 0:16, :]),
                                        e.dma_start(out=vbuf[0][:, 1:1 + nblk, :],
                                                    in_=Vd[h, 16:nkeys, :].rearrange("(b p) d -> p b d", p=128))]
                            S.op("sp", vfn, reads=vcells, writes=[("vbuf", 0)], dma=("vbuf", 0), ndma=2)
                        preA[len(jobs)] = kload
                        preC[len(jobs)] = vload
                        for ti in tis:
                            jobs += attn_jobs(hv, hl, ks, ti, g2 * 4 + hl, lrot.next())
                    run_pipeline(jobs, preA, preC)
                s0 = wget(uidx[pos]); pos += 1
                s1 = wget(uidx[pos], ahead=1); pos += 1
                out_proj([s0, s1], tis, ln=((L, 1) if half8 == 1 else None))

        for hv in halves:
            load_state(hv)
            for L in range(nlayers):
                tis = [0, 1, 2] if (hv == 0 and L < NCONV) else [1, 2]
                ffn((hv, L, "f0"), tis, L, 0)
                if L < NCONV:
                    conv_mixer(hv, L, (hv, L, "mix"), tis, L)
                else:
                    attn_mixer(hv, L - NCONV, (hv, L, "mix"), L)
                ffn((hv, L, "f1"), tis, L, 2)
                if L == NCONV - 1:
                    kv_proj(hv, (hv, L, "kv"), tis)
            store_out(hv)
        S.final_wait("sp", outcells)

        with nc.Block() as block:
            S.emit(block)
        stats = dict(nops=S.nops, ecnt=dict(S.ecnt), ndsem=len(S.dsem), words=off[0],
                     per_eng={e: len(v) for e, v in S.prog.items()})
    return nc, stats


def host_consts():
    ident = np.eye(128, dtype=np.float32)
    ones = np.ones((128, 128), np.float32)
    j = np.arange(128)[:, None]
    s = np.arange(128)[None, :]
    U = (j > s).astype(np.float32)
    tri = (s > j).astype(np.float32)
    return np.concatenate([ident, ones, -U, tri, -ones, ones], axis=1)


def make_in_maps(x, meta_tokens, ln_gain, ln_bias, ffn_w_in, ffn_w_out, conv_w_in, conv_w, conv_b,
                 conv_w_out, sb_w_q, sb_w_kv, sb_w_o, ncores=8):
    f = lambda a: np.ascontiguousarray(np.asarray(a, dtype=np.float32))
    prm = np.concatenate([f(ln_gain).reshape(192, 128), f(ln_bias).reshape(192, 128),
                          f(conv_w).reshape(96, 128), f(conv_b).reshape(32, 128)], axis=0)
    shared = {
        "meta": f(meta_tokens), "prm": np.ascontiguousarray(prm), "cst": host_consts(),
        "ffn_w_in": f(ffn_w_in).reshape(DEPTH * 2, D, 2 * DFF), "ffn_w_out": f(ffn_w_out).reshape(DEPTH * 2, DFF, D),
        "conv_w_in": f(conv_w_in), "conv_w_out": f(conv_w_out), "sb_w_q": f(sb_w_q), "sb_w_kv": f(sb_w_kv),
        "sb_w_o": f(sb_w_o),
    }
    xs = f(x)
    return [dict(shared, x=xs[b]) for b in range(ncores)]


def kernel(x, meta_tokens, ln_gain, ln_bias, ffn_w_in, ffn_w_out, conv_w_in, conv_w, conv_b,
           conv_w_out, sb_w_q, sb_w_kv, sb_w_o):
    nc, _ = build_program()
    in_maps = make_in_maps(x, meta_tokens, ln_gain, ln_bias, ffn_w_in, ffn_w_out, conv_w_in, conv_w, conv_b,
                           conv_w_out, sb_w_q, sb_w_kv, sb_w_o)
    res = run_bass_kernel_spmd(nc, in_maps, core_ids=list(range(8)))
    return np.stack([r["y"] for r in res.results], axis=0).astype(np.float32)
```
